# Optimizing a Trainium2 kernel written in Bass

```python
import jax, jax.numpy as jnp
from jax import lax
import numpy as np

D_MODEL = 2048
BATCH = 1
SEQ = 16384
DEPTH = 4
DEC_BATCH = 16
DEC_SEQ = 32
PAST_LEN = 4096

CHUNK = 64
D_MIX = D_MODEL
GROUP = D_MODEL // 16
N_POOL_G = 4
POOL_WINDOWS = (2, 4, 8, 16)
POOL_MAX = 16
D_POOL = N_POOL_G * GROUP
N_SGU_H = 6
D_SGU = N_SGU_H * GROUP
SGU_CHUNK = 128
N_CONV_G = 6
D_CONV = N_CONV_G * GROUP
CONV_K = 31
D_IN = D_POOL + 2 * D_SGU + 2 * D_CONV
SPLITS = (D_POOL, D_POOL + D_SGU, D_POOL + 2 * D_SGU, D_POOL + 2 * D_SGU + D_CONV)
D_FF = ((8 * D_MODEL // 3 + 255) // 256) * 256
EPS = 1e-6

kernel_name = 'hybrid_pool_sgu_conformer_stream_step'


def rms_norm(x, g):
    xf = x.astype(jnp.float32)
    y = xf * lax.rsqrt(jnp.mean(xf * xf, axis=-1, keepdims=True) + EPS)
    return (y * g.astype(jnp.float32)).astype(x.dtype)


def layer_norm(x, g, b):
    xf = x.astype(jnp.float32)
    mu = jnp.mean(xf, axis=-1, keepdims=True)
    xc = xf - mu
    y = xc * lax.rsqrt(jnp.mean(xc * xc, axis=-1, keepdims=True) + EPS)
    return (y * g.astype(jnp.float32) + b.astype(jnp.float32)).astype(x.dtype)


def swiglu(x, wg, wu, wd):
    return (jax.nn.silu(x @ wg) * (x @ wu)) @ wd


def pool_mixer(p, hist, pos0, pool_w, pool_scale):
    B, T, _ = p.shape
    H = hist.shape[1]
    seq = jnp.concatenate([hist, p], axis=1)
    padded = jnp.pad(seq.astype(jnp.float32), ((0, 0), (POOL_MAX, 0), (0, 0)))
    cs = jnp.cumsum(padded, axis=1)
    end = POOL_MAX + H
    cs_now = cs[:, end:end + T]
    pos = pos0 + jnp.arange(T)
    parts = []
    for g, w in enumerate(POOL_WINDOWS):
        sl = slice(g * GROUP, (g + 1) * GROUP)
        s = cs_now[..., sl] - cs[:, end - w:end - w + T, sl]
        cnt = jnp.minimum(w, pos + 1).astype(jnp.float32)[None, :, None]
        parts.append(s / cnt)
    pooled = jnp.concatenate(parts, axis=-1) - p.astype(jnp.float32)
    pooled = pooled.astype(p.dtype).reshape(B, T, N_POOL_G, GROUP)
    out = jnp.einsum('btgc,gcd->btgd', pooled, pool_w).reshape(B, T, D_POOL)
    return out * pool_scale, seq[:, -(POOL_MAX - 1):]


def sgu_mixer(u, v, sgu_w, sgu_b):
    B, T, _ = u.shape
    L = min(T, SGU_CHUNK)
    mask = jnp.tril(jnp.ones((L, L), dtype=bool))
    w = jnp.where(mask, sgu_w[:, :L, :L], 0)
    vb = v.reshape(B, T // L, L, N_SGU_H, GROUP)
    s = jnp.einsum('hqk,bnkhc->bnqhc', w, vb) + sgu_b[:, :L].T[None, None, :, :, None]
    return u * s.reshape(B, T, D_SGU)


def conv_mixer(za, zg, hist, conv_w, conv_b, ln_g, ln_b):
    c = za * jax.nn.sigmoid(zg)
    seq = jnp.concatenate([hist, c], axis=1)
    y = lax.conv_general_dilated(seq, conv_w[:, None, :], window_strides=(1,), padding='VALID',
                                 dimension_numbers=('NWC', 'WIO', 'NWC'),
                                 feature_group_count=D_CONV) + conv_b
    y = jax.nn.silu(layer_norm(y, ln_g, ln_b))
    return y, seq[:, -(CONV_K - 1):]


def layer(x, pool_hist, conv_hist, pos0, ng, wg, wu, wd, w_in_l, pool_w_l, pool_scale_l,
          sgu_g_l, sgu_w_l, sgu_b_l, conv_w_l, conv_b_l, ln_g_l, ln_b_l, w_out_l):
    x = x + 0.5 * rms_norm(swiglu(rms_norm(x, ng[0]), wg[0], wu[0], wd[0]), ng[1])
    h = rms_norm(x, ng[2])
    z = h @ w_in_l
    p, zu, zv, za, zg = jnp.split(z, SPLITS, axis=-1)
    y_pool, new_pool = pool_mixer(p, pool_hist, pos0, pool_w_l, pool_scale_l)
    u = jax.nn.gelu(zu, approximate=False)
    v = rms_norm(jax.nn.gelu(zv, approximate=False), sgu_g_l)
    y_sgu = sgu_mixer(u, v, sgu_w_l, sgu_b_l)
    y_conv, new_conv = conv_mixer(za, zg, conv_hist, conv_w_l, conv_b_l, ln_g_l, ln_b_l)
    m = jnp.concatenate([y_pool, y_sgu, y_conv], axis=-1) @ w_out_l
    x = x + rms_norm(m, ng[3])
    x = x + 0.5 * rms_norm(swiglu(rms_norm(x, ng[4]), wg[1], wu[1], wd[1]), ng[5])
    return x, new_pool, new_conv, v


def setup_inputs(seed: int = 0) -> dict:
    key = jax.random.key(seed)
    ks = jax.random.split(key, 19)

    def nrm(k, shape, s):
        return jax.random.normal(k, shape, jnp.float32) * s

    return {
        'x_prompt': nrm(ks[0], (BATCH, SEQ, D_MODEL), 1.0),
        'x_sample': nrm(ks[1], (DEC_BATCH, DEC_SEQ, D_MODEL), 1.0),
        'state_pool': nrm(ks[2], (DEPTH, DEC_BATCH, POOL_MAX - 1, D_POOL), 1.0),
        'state_conv': nrm(ks[3], (DEPTH, DEC_BATCH, CONV_K - 1, D_CONV), 0.5),
        'norm_g': 1.0 + nrm(ks[4], (DEPTH, 6, D_MODEL), 0.05),
        'ffn_w_gate': nrm(ks[5], (DEPTH, 2, D_MODEL, D_FF), D_MODEL ** -0.5),
        'ffn_w_up': nrm(ks[6], (DEPTH, 2, D_MODEL, D_FF), D_MODEL ** -0.5),
        'ffn_w_down': nrm(ks[7], (DEPTH, 2, D_FF, D_MODEL), D_FF ** -0.5),
        'w_in': nrm(ks[8], (DEPTH, D_MODEL, D_IN), D_MODEL ** -0.5),
        'pool_w': nrm(ks[9], (DEPTH, N_POOL_G, GROUP, GROUP), GROUP ** -0.5),
        'pool_scale': 1.0 + nrm(ks[10], (DEPTH, D_POOL), 0.1),
        'sgu_norm_g': 1.0 + nrm(ks[11], (DEPTH, D_SGU), 0.05),
        'sgu_w': nrm(ks[12], (DEPTH, N_SGU_H, SGU_CHUNK, SGU_CHUNK), SGU_CHUNK ** -0.5),
        'sgu_b': 1.0 + nrm(ks[13], (DEPTH, N_SGU_H, SGU_CHUNK), 0.1),
        'conv_w': nrm(ks[14], (DEPTH, CONV_K, D_CONV), CONV_K ** -0.5),
        'conv_b': nrm(ks[15], (DEPTH, D_CONV), 0.02),
        'conv_ln_g': 1.0 + nrm(ks[16], (DEPTH, D_CONV), 0.05),
        'conv_ln_b': nrm(ks[17], (DEPTH, D_CONV), 0.02),
        'w_out': nrm(ks[18], (DEPTH, D_MIX, D_MODEL), D_MIX ** -0.5),
    }


def reference(x_prompt, x_sample, state_pool, state_conv, norm_g, ffn_w_gate, ffn_w_up, ffn_w_down,
              w_in, pool_w, pool_scale, sgu_norm_g, sgu_w, sgu_b, conv_w, conv_b, conv_ln_g, conv_ln_b,
              w_out):
    yp = x_prompt
    ys = x_sample
    bp = x_prompt.shape[0]
    pool_p, conv_p, pool_s, conv_s, v_s = [], [], [], [], []
    for l in range(DEPTH):
        lw = (norm_g[l], ffn_w_gate[l], ffn_w_up[l], ffn_w_down[l], w_in[l], pool_w[l], pool_scale[l],
              sgu_norm_g[l], sgu_w[l], sgu_b[l], conv_w[l], conv_b[l], conv_ln_g[l], conv_ln_b[l], w_out[l])
        yp, pp, cp, _ = layer(yp, jnp.zeros((bp, 0, D_POOL), yp.dtype),
                              jnp.zeros((bp, CONV_K - 1, D_CONV), yp.dtype), 0, *lw)
        ys, ps, cs, vs = layer(ys, state_pool[l], state_conv[l], PAST_LEN, *lw)
        pool_p.append(pp)
        conv_p.append(cp)
        pool_s.append(ps)
        conv_s.append(cs)
        v_s.append(vs)
    new_pool_prompt = jnp.stack(pool_p)
    new_conv_prompt = jnp.stack(conv_p)
    new_pool_sample = jnp.stack(pool_s)
    new_conv_sample = jnp.stack(conv_s)
    new_sgu_v_sample = jnp.stack(v_s)
    return (yp, ys, new_pool_prompt, new_conv_prompt, new_pool_sample, new_conv_sample, new_sgu_v_sample)
```

```python
import numpy as np
import concourse.bass as bass
import concourse.mybir as mybir
from concourse.bass_utils import run_bass_kernel_spmd
from contextlib import ExitStack

F32 = mybir.dt.float32
BF16 = mybir.dt.bfloat16
AF = mybir.ActivationFunctionType
ALU = mybir.AluOpType

D = 2048
DFF = 5632
DIN = 3584
NKC = 16
NHC = 44
WU = 384
EPS = 1e-6
GR = 64
PW = 544
NCST = 320
POOL_W = (2, 4, 8, 16)


class V:
    __slots__ = ("ap", "res")

    def __init__(self, ap, res):
        self.ap = ap
        self.res = list(res)

    def c(self, a, b):
        return V(self.ap[:, a:b], self.res)

    def p(self, a, b):
        return V(self.ap[a:b], self.res)


class Op:
    __slots__ = ("eng", "fn", "idx", "deps", "sig", "semval", "dma", "slot", "dval")

    def __init__(self, eng, fn, dma):
        self.eng = eng
        self.fn = fn
        self.dma = dma
        self.deps = []
        self.sig = False
        self.semval = 0
        self.slot = -1
        self.dval = 0


class Prog:
    ENGS = ("pe", "act", "dve", "pool", "sp")
    NDS = {"sp": 24, "pool": 8}

    def __init__(self):
        self.ops = {e: [] for e in self.ENGS}
        self.wr_c = {}
        self.wr_d = {}
        self.rd_c = {}
        self.rd_d = {}
        self.ndma = {"sp": 0, "pool": 0}
        self.dma_ops = {"sp": [], "pool": []}

    def add(self, eng, fn, rd=(), wr=(), dma=False):
        op = Op(eng, fn, dma)
        lst = self.ops[eng]
        op.idx = len(lst)
        lst.append(op)
        deps = {}
        rres = set()
        for v in rd:
            rres.update(v.res)
        wres = set()
        for v in wr:
            wres.update(v.res)
        psr = [r for r in rres if isinstance(r, tuple) and r[0] == "ps"]
        for r in psr:
            rres.discard(r)
            wres.add(r)

        def dep(o):
            deps[id(o)] = o

        for r in rres:
            w = self.wr_c.get(r)
            if w:
                for o in w.values():
                    dep(o)
            w = self.wr_d.get(r)
            if w is not None:
                dep(w)
        for r in wres:
            w = self.wr_c.get(r)
            if w:
                for o in w.values():
                    dep(o)
            w = self.wr_d.get(r)
            if w is not None:
                dep(w)
            w = self.rd_c.get(r)
            if w:
                for o in w.values():
                    dep(o)
            w = self.rd_d.get(r)
            if w:
                for o in w:
                    dep(o)
        for o in deps.values():
            if o.dma:
                op.deps.append(o)
            elif o.eng == eng and not dma:
                if eng == "pe":
                    continue
                if op.idx - o.idx > 1:
                    continue
                o.sig = True
                op.deps.append(o)
            else:
                o.sig = True
                op.deps.append(o)
        if dma:
            q = eng
            j = self.ndma[q]
            self.ndma[q] = j + 1
            op.slot = j % self.NDS[q]
            op.dval = 16 * (j // self.NDS[q] + 1)
            self.dma_ops[q].append(op)
            for r in rres:
                self.rd_d.setdefault(r, []).append(op)
            for r in wres:
                self.wr_d[r] = op
                self.wr_c.pop(r, None)
                self.rd_c.pop(r, None)
                self.rd_d.pop(r, None)
        else:
            for r in rres:
                self.rd_c.setdefault(r, {})[eng] = op
            for r in wres:
                self.wr_c[r] = {eng: op}
                self.wr_d.pop(r, None)
                self.rd_c.pop(r, None)
                self.rd_d.pop(r, None)
        return op

    def emit(self, nc, es):
        sems = {e: es.enter_context(nc.semaphore("s_" + e)) for e in ("pe", "act", "dve", "pool")}
        dsem = {q: [es.enter_context(nc.semaphore("d_%s%d" % (q, i))) for i in range(n)]
                for q, n in self.NDS.items()}
        for e in ("pe", "act", "dve", "pool"):
            cnt = 0
            for op in self.ops[e]:
                if op.sig and not op.dma:
                    cnt += 1
                    op.semval = cnt
        block = es.enter_context(nc.Block())
        prog = self

        def run(ename, eng):
            waited = {}

            def wait(key, sem, val):
                if waited.get(key, 0) >= val:
                    return
                waited[key] = val
                eng.wait_ge(sem, val)

            for op in prog.ops[ename]:
                for o in op.deps:
                    if o.dma:
                        wait(("d", o.eng, o.slot), dsem[o.eng][o.slot], o.dval)
                    else:
                        wait(("c", o.eng), sems[o.eng], o.semval)
                if op.dma:
                    if op.dval > 16:
                        wait(("d", ename, op.slot), dsem[ename][op.slot], op.dval - 16)
                    inst = op.fn(eng)
                    inst.then_inc(dsem[ename][op.slot], 16)
                else:
                    inst = op.fn(eng)
                    if op.sig:
                        inst.then_inc(sems[ename], 1)
            if ename == "sp":
                for q in ("sp", "pool"):
                    last = {}
                    for o in prog.dma_ops[q]:
                        last[o.slot] = o.dval
                    for s, v in last.items():
                        eng.wait_ge(dsem[q][s], v)

        @block.tensor
        def _(e):
            run("pe", e)

        @block.scalar
        def _(e):
            run("act", e)

        @block.vector
        def _(e):
            run("dve", e)

        @block.gpsimd
        def _(e):
            run("pool", e)

        @block.sync
        def _(e):
            run("sp", e)


def build(cfg):
    NL = cfg["NL"]
    TPC = cfg["TPC"]
    NPT = TPC // 512
    CASTALL = cfg.get("castall", True)
    PECONV = cfg.get("peconv", True)
    nc = bass.Bass("TRN2", target_bir_lowering=False)
    P = Prog()

    def din(name, shape, dt=F32):
        return nc.dram_tensor(name, list(shape), dt, kind="ExternalInput").ap()

    def dout(name, shape):
        return nc.dram_tensor(name, list(shape), F32, kind="ExternalOutput").ap()

    def dscr(name, shape, dt=BF16):
        return nc.dram_tensor(name, list(shape), dt, kind="Internal").ap()

    LAY2 = cfg.get("lay2", True)
    WUR = 256 if LAY2 else WU
    xT = din("xT", [D, WUR + TPC])
    xsT = din("xsT", [D, 64])
    spT = din("spT", [NL, 2, 512, 15])
    scT = din("scT", [NL, 2, 768, 30])
    flag_d = din("flag", [128, 1])
    invc_d = din("invc", [128, 64])
    cst_d = din("cst", [NL, 128, NCST])
    swT_d = din("swT", [NL, 128, 6, 128])
    sb_d = din("sb", [NL, 1, 768])
    mask_d = din("mask", [128, 128])
    identb_d = din("ident", [128, 128])
    wg_d = din("wg", [NL, 2, D, DFF])
    wu_d = din("wu", [NL, 2, D, DFF])
    wd_d = din("wd", [NL, 2, DFF, D])
    win_d = din("win", [NL, D, DIN])
    wout_d = din("wout", [NL, D, D])
    pw_d = din("pw", [NL, 4, 128, 128])

    yT = dout("yT", [D, TPC])
    ysT = dout("ysT", [D, 64])
    nppT = dout("nppT", [NL, 512, 16])
    ncpT = dout("ncpT", [NL, 768, 32])
    npsT = dout("npsT", [NL, 2, 512, 16])
    ncsT = dout("ncsT", [NL, 2, 768, 32])
    nvT = dout("nvT", [NL, 768, 64])

    wgb = dscr("wgb", [NL, 2, D, DFF])
    wub = dscr("wub", [NL, 2, D, DFF])
    wdb = dscr("wdb", [NL, 2, DFF, D])
    winb = dscr("winb", [NL, D, DIN])
    woutb = dscr("woutb", [NL, D, D])
    pwb_s = dscr("pwb", [NL, 4, 128, 128])

    es = ExitStack()
    cols = [0]

    def alloc(n):
        n = (n + GR - 1) // GR * GR
        c0 = cols[0]
        cols[0] += n
        return c0

    X0 = alloc(NKC * 512)
    HD0 = alloc(NKC * 512)
    R0 = alloc(NHC * 256)
    PB0 = alloc(4 * PW)
    PA0 = alloc(2 * PW)
    T20 = alloc(2 * 512)
    CB0 = alloc(3 * PW)
    CBB0 = alloc(2 * 272)
    VF0 = alloc(6 * 64)
    HP0 = alloc(NL * 64)
    HC0 = alloc(NL * 192)
    SQ0 = alloc(3 * 256)
    RS0 = alloc(2 * 512)
    MU0 = alloc(2 * 512)
    VT0 = alloc(4 * 384)
    CST0 = alloc(NCST)
    PWB0 = alloc(256)
    SWT0 = alloc(768)
    WMT0 = alloc(384)
    SWS0 = alloc(384)
    WMS0 = alloc(192)
    BR0 = alloc(768)
    BR20 = alloc(384)
    STC0 = alloc(384)
    STP0 = alloc(128)
    ONB0 = alloc(64)
    IDB0 = alloc(64)
    MSK0 = alloc(128)
    ONR0 = alloc(128)
    FLG0 = alloc(64)
    INV0 = alloc(64)
    IDF0 = alloc(128)
    EPS0 = alloc(64)
    NWS = cfg.get("NWS", 4)
    WR0 = alloc(NWS * 2048)
    TOT = cols[0]
    assert TOT * 4 <= 207 * 1024, TOT * 4
    print("arena bytes/partition", TOT * 4)
    arena_t = es.enter_context(nc.sbuf_tensor("arena", [128, TOT], F32))
    arena = arena_t[:]
    pst = [es.enter_context(nc.psum_tensor("ps%d" % i, [128, 512], F32)) for i in range(8)]

    def gres(c0, n):
        return range(c0 // GR, (c0 + n - 1) // GR + 1)

    def Af(c0, n, p1=128):
        return V(arena[0:p1, c0:c0 + n], gres(c0, n))

    def Ab(c0, n, p1=128):
        nc_ = (n + 1) // 2
        return V(arena[0:p1, c0:c0 + nc_].bitcast(BF16), gres(c0, nc_))

    def PS(b, a=0, e=512, p1=128):
        return V(pst[b][0:p1, a:e], [("ps", b)])

    def DR(ap, *names):
        return V(ap, names)

    C0 = [0]

    def Xc(k, n):
        return Af(X0 + k * 512, n).c(C0[0], n)

    def Dc(m, n):
        return Af(HD0 + m * 512, n).c(C0[0], n)

    def Hc(k, n):
        return Ab(HD0 + k * 512, n).c(C0[0], n)

    def HIDc(j, n):
        return Ab(R0 + j * 256, n).c(C0[0], n)

    def MIXc(i, n):
        return Ab(R0 + i * 256, n).c(C0[0], n)

    def MIXf(i, n):
        return Ab(R0 + i * 256, n)

    def PSn(b, n):
        return PS(b, C0[0], n)

    ACC0 = R0 + 16 * 256
    GV0 = ACC0 + 6 * PW
    assert GV0 + 6 * 512 <= R0 + NHC * 256

    def ACCc(i):
        return Af(ACC0 + i * PW, PW)

    def GVc(i, n):
        return Af(GV0 + i * 512, n).c(C0[0], n)

    def VFMc(i, n):
        return Ab(HD0 + i * 512 + 256, n).c(C0[0], n)

    def POOLEDc(g, n):
        return Ab(HD0 + (6 + g) * 512 + 256, n)

    def VTMc(tc, p1=128):
        return Ab(VT0 + tc * 384, 768, p1)

    def T2(i, n):
        return Af(T20 + (i % 2) * 512, n).c(C0[0], n)

    def SQ(i, n):
        return Ab(SQ0 + (i % 3) * 256, n).c(C0[0], n)

    def RS(i, n):
        return Af(RS0 + (i % 2) * 512, n).c(C0[0], n)

    def SQf(i, n):
        return Ab(SQ0 + (i % 3) * 256, n)

    def RSf(i, n):
        return Af(RS0 + (i % 2) * 512, n)

    ONESB = Ab(ONB0, 128)
    IDENTB = Ab(IDB0, 128)
    MASK = Af(MSK0, 128)
    ONEROW = Af(ONR0, 128, 1)
    FLAG = Af(FLG0, 1)
    INVC = Af(INV0, 64)
    EPSC = Af(EPS0, 1)

    cnt = {"bank": 0, "stat": 0, "sq": 0, "rs": 0, "t2": 0, "w": 0, "cb": 0}

    def newbank():
        b = cnt["bank"] % 6
        cnt["bank"] += 1
        return b

    def newstat():
        b = 6 + cnt["stat"] % 2
        cnt["stat"] += 1
        return b

    def dma(q, out, in_):
        P.add(q, lambda e, o=out.ap, i=in_.ap: e.dma_start(out=o, in_=i), rd=[in_], wr=[out], dma=True)

    def act(out, in_, func, scale=None, bias=None, extra_rd=()):
        kw = {}
        if scale is not None:
            kw["scale"] = scale
        if bias is not None:
            kw["bias"] = bias
        P.add("act", lambda e, o=out.ap, i=in_.ap, f=func, kw=kw: e.activation(out=o, in_=i, func=f, **kw),
              rd=[in_] + list(extra_rd), wr=[out])

    def tt(out, a, b, op, eng="dve"):
        P.add(eng, lambda e, o=out.ap, x=a.ap, y=b.ap, op=op: e.tensor_tensor(out=o, in0=x, in1=y, op=op),
              rd=[a, b], wr=[out])

    def ts(out, a, s1, s2, op0, op1=None, extra_rd=(), eng="dve"):
        def fn(e, o=out.ap, x=a.ap):
            if op1 is None:
                return e.tensor_scalar(out=o, in0=x, scalar1=s1, scalar2=None, op0=op0)
            return e.tensor_scalar(out=o, in0=x, scalar1=s1, scalar2=s2, op0=op0, op1=op1)
        P.add(eng, fn, rd=[a] + list(extra_rd), wr=[out])

    def stt(out, a, s, b, op0, op1, extra_rd=(), eng="dve"):
        P.add(eng, lambda e, o=out.ap, x=a.ap, y=b.ap: e.scalar_tensor_tensor(out=o, in0=x, scalar=s, in1=y, op0=op0, op1=op1),
              rd=[a, b] + list(extra_rd), wr=[out])

    def cp(out, in_, eng="dve"):
        P.add(eng, lambda e, o=out.ap, i=in_.ap: e.tensor_copy(out=o, in_=i), rd=[in_], wr=[out])

    def recip(out, in_):
        P.add("dve", lambda e, o=out.ap, i=in_.ap: e.reciprocal(out=o, in_=i), rd=[in_], wr=[out])

    def memset(out, val, eng="dve"):
        P.add(eng, lambda e, o=out.ap: e.memset(o, val), rd=[], wr=[out])

    def mm_group(outs, pairs, rd, first=True, last=True):
        seen = set()
        plan = []
        lastidx = {}
        for n, (oi, l, r) in enumerate(pairs):
            lastidx[oi] = n
        for n, (oi, l, r) in enumerate(pairs):
            st = first and (oi not in seen)
            seen.add(oi)
            sp = last and lastidx[oi] == n
            plan.append((outs[oi].ap, l, r, st, sp))

        def fn(e):
            inst = None
            for (o, l, r, st, sp) in plan:
                inst = e.matmul(o, lhsT=l, rhs=r, start=st, stop=sp)
            return inst
        P.add("pe", fn, rd=rd, wr=outs)

    memset(Af(PB0, 4 * PW), 0.0)
    memset(Af(PA0, 2 * PW), 0.0)
    memset(Af(CB0, 3 * PW), 0.0)
    memset(Af(SWS0, 384), 0.0)
    memset(Af(HP0, NL * 64), 0.0)
    memset(Af(HC0, NL * 192), 0.0)
    memset(Af(R0, NHC * 256), 0.0)
    memset(Af(STC0, 384), 0.0)
    memset(Af(STP0, 128), 0.0)
    memset(ONESB, 1.0)
    memset(ONEROW, 1.0)
    memset(EPSC, EPS)
    dma("sp", MASK, DR(mask_d[:, :], "mask_d"))
    dma("sp", Af(IDF0, 128), DR(identb_d[:, :], "ident_d"))
    dma("sp", FLAG, DR(flag_d[:, :], "flag_d"))
    dma("sp", INVC, DR(invc_d[:, :], "invc_d"))
    cp(IDENTB, Af(IDF0, 128))

    def conv_mat(src, dst, name, nrows, rstep):
        for r0 in range(0, nrows, rstep):
            r1 = min(nrows, r0 + rstep)
            dma("pool", DR(dst[r0:r1, :], (name, r0 // rstep)), DR(src[r0:r1, :], name + "_src"))

    def wres(name, nrows, rstep, r0=0, r1=None):
        r1 = nrows if r1 is None else r1
        return [(name, i) for i in range(r0 // rstep, (r1 - 1) // rstep + 1)]

    convlist = {}

    def conv_list(src, dst, name, nrows, rstep):
        out = []
        for r0 in range(0, nrows, rstep):
            r1 = min(nrows, r0 + rstep)
            out.append((DR(dst[r0:r1, :], (name, r0 // rstep)), DR(src[r0:r1, :], name + "_src")))
        return out

    for l in range(NL):
        for f in range(2):
            convlist[l * 3 + 2 * f] = (conv_list(wg_d[l, f], wgb[l, f], "wg%d_%d" % (l, f), D, 128)
                                       + conv_list(wu_d[l, f], wub[l, f], "wu%d_%d" % (l, f), D, 128)
                                       + conv_list(wd_d[l, f], wdb[l, f], "wd%d_%d" % (l, f), DFF, 256))
        convlist[l * 3 + 1] = (conv_list(win_d[l], winb[l], "win%d" % l, D, 128)
                               + [(DR(pwb_s[l].rearrange("g c d -> (g c) d"), ("pw%d" % l, 0)),
                                   DR(pw_d[l].rearrange("g c d -> (g c) d"), "pw_src"))]
                               + conv_list(wout_d[l], woutb[l], "wout%d" % l, D, 256))
    if CASTALL:
        convlist = {}
        wgS, wuS, wdS, winS, woutS = wg_d, wu_d, wd_d, win_d, wout_d
    else:
        wgS, wuS, wdS, winS, woutS = wgb, wub, wdb, winb, woutb
    WQ = {"q": "pool", "pend": [], "left": 0}

    def conv_emit(k):
        for _ in range(k):
            if not WQ["pend"]:
                return
            o, i = WQ["pend"].pop(0)
            dma("pool", o, i)

    def stage_begin(sid, nloads):
        if WQ["q"] != "pool":
            return
        WQ["pend"].extend(convlist.get(sid + 2, []))
        WQ["left"] = nloads

    def stage_end():
        if WQ["q"] != "pool":
            return
        conv_emit(len(WQ["pend"]))

    for sid in (0, 1):
        WQ["pend"].extend(convlist.get(sid, []))
    conv_emit(len(WQ["pend"]))

    def wload(src_ap, nk, names):
        s = cnt["w"] % NWS
        cnt["w"] += 1
        v = Ab(WR0 + s * 2048, nk * 256)
        ap3 = v.ap.rearrange("p (k n) -> p k n", n=256)
        dst = V(ap3, v.res)
        dma(WQ["q"], dst, DR(src_ap.rearrange("(k p) n -> p k n", p=128), *names))
        if WQ["q"] == "pool":
            left = max(1, WQ["left"])
            conv_emit((len(WQ["pend"]) + left - 1) // left)
            WQ["left"] -= 1
        return v, ap3

    class Stat:
        def __init__(self, n, dim):
            self.n = n
            self.bank = newstat()
            self.i = 0
            self.dim = dim

        def add(self, src, last):
            s = SQ(cnt["sq"], self.n)
            cnt["sq"] += 1
            act(s, src, AF.Square)
            mm_group([PSn(self.bank, self.n)], [(0, ONESB.ap, s.ap)], rd=[ONESB, s],
                     first=(self.i == 0), last=last)
            self.i += 1

        def finish(self):
            r = RS(cnt["rs"], self.n)
            cnt["rs"] += 1
            act(r, PSn(self.bank, self.n), AF.Sqrt, scale=1.0 / self.dim, bias=EPSC.ap[:, 0:1], extra_rd=[EPSC])
            recip(r, r)
            return r

    def prenorm(gcol, n, pk=False):
        st = Stat(n, D)
        for k in range(NKC):
            st.add(Xc(k, n), k == NKC - 1)
        plvl = cfg.get("plvl", 9)
        if plvl < 1:
            return
        r = st.finish()
        if plvl < 2:
            return
        CST = Af(CST0, NCST)
        for k in range(NKC):
            stt(Hc(k, n), Xc(k, n), CST.ap[:, gcol * 16 + k:gcol * 16 + k + 1], r, ALU.mult, ALU.mult, extra_rd=[CST],
                eng=("pool" if (pk and k % 3 == 2) else "dve"))

    def postnorm(st, gcol, n, half, pk=False):
        r = st.finish()
        CST = Af(CST0, NCST)
        for m in range(NKC):
            stt(Dc(m, n), Dc(m, n), CST.ap[:, gcol * 16 + m:gcol * 16 + m + 1], r, ALU.mult, ALU.mult, extra_rd=[CST])
        for m in range(NKC):
            stt(Xc(m, n), Dc(m, n), half, Xc(m, n), ALU.mult, ALU.add)

    def ffn(l, f, n, pk=False):
        lvl = cfg.get("lvl", 9)
        prenorm(0 if f == 0 else 4, n, pk)
        if lvl < 2:
            return
        wgn = wres("wg%d_%d" % (l, f), D, 128)
        wun = wres("wu%d_%d" % (l, f), D, 128)
        hall = [Hc(k, n) for k in range(NKC)]
        for jg in range(NHC // 2):
            vg, ag = wload(wgS[l, f][:, jg * 256:(jg + 1) * 256], NKC, wgn)
            vu, au = wload(wuS[l, f][:, jg * 256:(jg + 1) * 256], NKC, wun)
            kb = None
            if jg == 0 and cfg.get("kouter", True):
                kb = [newbank() for _ in range(4)]
                kouts = [PSn(b_, n) for b_ in kb]
                for k in range(NKC):
                    mm_group(kouts, [(0, ag[:, k, 0:128], hall[k].ap), (1, au[:, k, 0:128], hall[k].ap),
                                     (2, ag[:, k, 128:256], hall[k].ap), (3, au[:, k, 128:256], hall[k].ap)],
                             rd=[vg, vu, hall[k]], first=(k == 0), last=(k == NKC - 1))
            for jj in range(2):
                j = 2 * jg + jj
                if kb is not None:
                    bg, bu = kb[2 * jj], kb[2 * jj + 1]
                else:
                    bg = newbank()
                    mm_group([PSn(bg, n)], [(0, ag[:, k, jj * 128:(jj + 1) * 128], hall[k].ap) for k in range(NKC)],
                             rd=[vg] + hall)
                    bu = newbank()
                    mm_group([PSn(bu, n)], [(0, au[:, k, jj * 128:(jj + 1) * 128], hall[k].ap) for k in range(NKC)],
                             rd=[vu] + hall)
                t = T2(cnt["t2"], n)
                cnt["t2"] += 1
                act(t, PSn(bg, n), AF.Silu)
                tt(HIDc(j, n), t, PSn(bu, n), ALU.mult)
        if lvl < 3:
            return
        st = Stat(n, D)
        parts = [(0, 16), (16, 32), (32, 44)]
        for mg in range(NKC // 2):
            b = [newbank(), newbank()]
            outs = [PSn(b[0], n), PSn(b[1], n)]
            for pi, (k0, k1) in enumerate(parts):
                vw, aw = wload(wdS[l, f][k0 * 128:k1 * 128, mg * 256:(mg + 1) * 256], k1 - k0,
                               wres("wd%d_%d" % (l, f), DFF, 256, k0 * 128, k1 * 128))
                hs = [HIDc(k, n) for k in range(k0, k1)]
                pairs = []
                for k in range(k0, k1):
                    for mm in range(2):
                        pairs.append((mm, aw[:, k - k0, mm * 128:(mm + 1) * 128], hs[k - k0].ap))
                mm_group(outs, pairs, rd=[vw] + hs, first=(pi == 0), last=(pi == 2))
            for mm in range(2):
                m = 2 * mg + mm
                if cfg.get("dk", 3) & 1:
                    st.add(outs[mm], m == NKC - 1)
                if cfg.get("dk", 3) & 2:
                    cp(Dc(m, n), outs[mm])
        if lvl < 4:
            return
        postnorm(st, 1 if f == 0 else 5, n, 0.5, pk)

    def mixer(l, tile):
        n = tile["n"]
        pk = False
        segs = tile["segs"]
        CST = Af(CST0, NCST)

        def cst(c):
            return CST.ap[:, c:c + 1]
        C_PSC, C_SGG, C_CW, C_CB, C_LG, C_LB = 96, 100, 106, 292, 298, 304
        prenorm(2, n, pk)
        hall = [Hc(k, n) for k in range(NKC)]
        winn = wres("win%d" % l, D, 128)

        def win_slot(s):
            return wload(winS[l][:, s * 256:(s + 1) * 256], NKC, winn)

        def zmm(vw, aw, jj):
            b = newbank()
            mm_group([PSn(b, n)], [(0, aw[:, k, jj * 128:(jj + 1) * 128], hall[k].ap) for k in range(NKC)],
                     rd=[vw] + hall)
            return b

        def seg_pd(sg):
            return sg["pd"]

        pbanks = []
        for q in range(2):
            vp, ap_ = win_slot(q)
            for jj in range(2):
                pbanks.append(zmm(vp, ap_, jj))
        pwn = [("pw%d" % l, 0)]
        for g in range(4):
            pb = Af(PB0 + g * PW, PW)
            for sg in segs:
                pd, t0, ln = sg["pd"], sg["t0"], sg["len"]
                act(pb.c(pd, pd + ln), PS(pbanks[g], t0, t0 + ln), AF.Copy)
                if sg["kind"] == "pr":
                    cp(pb.c(pd - 16, pd), Af(HP0 + l * 64 + g * 16, 16))
                elif sg["kind"] == "wu":
                    memset(pb.c(pd - 16, pd), 0.0)
                elif sg["kind"] == "pr0":
                    ts(pb.c(pd - 16, pd), pb.c(pd - 48, pd - 32), FLAG.ap[:, 0:1], None, ALU.mult, extra_rd=[FLAG])
                else:
                    cp(pb.c(pd - 16, pd), Af(STP0 + (g * 2 + sg["s"]) * 16, 16))
            for sg in segs:
                pd, ln = sg["pd"], sg["len"]
                src = pb.c(pd + ln - 16, pd + ln)
                if sg["kind"] in ("pr", "pr0"):
                    cp(Af(HP0 + l * 64 + g * 16, 16), src)
                elif sg["kind"] == "wu":
                    if not LAY2:
                        ts(Af(HP0 + l * 64 + g * 16, 16), src, FLAG.ap[:, 0:1], None, ALU.mult, extra_rd=[FLAG])
                else:
                    dma("sp", DR(npsT[l, sg["s"], g * 128:(g + 1) * 128, :], ("nps", l, sg["s"], g)), src)
            bufs = [pb, Af(PA0, PW), Af(PA0 + PW, PW)]
            cur = 0
            sh = 1
            for step in range(g + 1):
                nxt = 1 if cur != 1 else 2
                lo = 2 * sh - 1
                tt(bufs[nxt].c(lo, PW), bufs[cur].c(lo, PW), bufs[cur].c(lo - sh, PW - sh), ALU.add)
                cur = nxt
                sh *= 2
            w = POOL_W[g]
            pooled = POOLEDc(g, 512)
            stt(pooled, bufs[cur].c(32, PW), 1.0 / w, pb.c(32, PW), ALU.mult, ALU.subtract)
            fp = tile.get("first_pd")
            if fp:
                tmp = Af(MU0, 16)
                tt(tmp, bufs[cur].c(fp, fp + 16), INVC.c(g * 16, g * 16 + 16), ALU.mult)
                tt(pooled.c(fp - 32, fp - 16), tmp, pb.c(fp, fp + 16), ALU.subtract)
            b = newbank()
            pwv = Ab(PWB0, 512)
            mm_group([PS(b, 0, 512)], [(0, pwv.ap[:, g * 128:(g + 1) * 128], pooled.ap)], rd=[pwv, pooled])
            for sg in segs:
                a0 = sg["pd"] - 32
                t0, ln = sg["t0"], sg["len"]
                ts(MIXf(g, n).c(t0, t0 + ln), PS(b, a0, a0 + ln), cst(C_PSC + g), None, ALU.mult, extra_rd=[CST])

        st = Stat(n, 768)
        for q in range(3):
            vv, av = win_slot(5 + q)
            for jj in range(2):
                c_ = 2 * q + jj
                b = zmm(vv, av, jj)
                act(GVc(c_, n), PSn(b, n), AF.Gelu)
                st.add(GVc(c_, n), c_ == 5)
        r = st.finish()
        for c_ in range(6):
            stt(VFMc(c_, n), GVc(c_, n), cst(C_SGG + c_), r, ALU.mult, ALU.mult, extra_rd=[CST])
            if tile.get("samp"):
                vf = Af(VF0 + c_ * 64, 64)
                stt(vf, GVc(c_, n).c(WU - C0[0], WU + 64 - C0[0]), cst(C_SGG + c_), r.c(WU - C0[0], WU + 64 - C0[0]), ALU.mult, ALU.mult, extra_rd=[CST])
                dma("sp", DR(nvT[l, c_ * 128:(c_ + 1) * 128, :], ("nv", l, c_)), vf)
        tcs = tile["tcs"]
        for ti, (t0, ntok) in enumerate(tcs):
            b0 = newbank()
            b1 = newbank()
            outs = [PS(b0, 0, 512, ntok), PS(b1, 0, 256, ntok)]
            pairs = []
            for c_ in range(6):
                o = outs[0].ap[:, c_ * 128:(c_ + 1) * 128] if c_ < 4 else outs[1].ap[:, (c_ - 4) * 128:(c_ - 3) * 128]
                pairs.append((o, VFMc(c_, n).ap[:, t0 - C0[0]:t0 - C0[0] + ntok]))
            plan = list(pairs)

            def fn(e, plan=plan):
                inst = None
                for (o, l_) in plan:
                    inst = e.matmul(o, lhsT=l_, rhs=IDENTB.ap, start=True, stop=True)
                return inst
            P.add("pe", fn, rd=[IDENTB] + [VFMc(c_, n) for c_ in range(6)], wr=outs)
            vt = VTMc(ti, ntok)
            act(V(vt.ap[:, 0:512], vt.res), outs[0], AF.Copy)
            cp(V(vt.ap[:, 512:768], vt.res), outs[1])
        WMT = Ab(WMT0, 768)
        WMS = Ab(WMS0, 384, 64)
        BROW = Af(BR0, 768, 1)
        BROW2 = Af(BR20, 384, 1)
        for q in range(3):
            vz, az = win_slot(2 + q)
            for jj in range(2):
                h = 2 * q + jj
                bzu = zmm(vz, az, jj)
                u = T2(cnt["t2"], n)
                cnt["t2"] += 1
                act(u, PSn(bzu, n), AF.Gelu)
                bs = newbank()
                plan = []
                rds = [ONEROW, BROW, BROW2, WMT, WMS]
                for ti, (t0, ntok) in enumerate(tcs):
                    vt = VTMc(ti, ntok)
                    rds.append(vt)
                    if ntok == 128:
                        plan.append((pst[bs][:, t0:t0 + 128], ONEROW.ap, BROW.ap[:, h * 128:(h + 1) * 128], True, False))
                        plan.append((pst[bs][:, t0:t0 + 128], vt.ap[:, h * 128:(h + 1) * 128], WMT.ap[:, h * 128:(h + 1) * 128], False, True))
                    else:
                        plan.append((pst[bs][:, t0:t0 + 64], ONEROW.ap, BROW2.ap[:, h * 64:(h + 1) * 64], True, False))
                        plan.append((pst[bs][:, t0:t0 + 64], vt.ap[:, h * 128:(h + 1) * 128], WMS.ap[:, h * 64:(h + 1) * 64], False, True))

                def fn(e, plan=plan):
                    inst = None
                    for (o, l_, r_, st_, sp_) in plan:
                        inst = e.matmul(o, lhsT=l_, rhs=r_, start=st_, stop=sp_)
                    return inst
                P.add("pe", fn, rd=rds, wr=[PSn(bs, n)])
                tt(MIXc(4 + h, n), u, PSn(bs, n), ALU.mult)

        for q in range(3):
            va, aa = win_slot(8 + q)
            vg_, ag_ = win_slot(11 + q)
            dsets = []
            if PECONV:
                for jj in range(2):
                    i = 2 * q + jj
                    s_ = cnt["w"] % NWS
                    cnt["w"] += 1
                    dv = Ab(WR0 + s_ * 2048, 31 * 128)
                    if cfg.get("diag1", True):
                        o3 = dv.ap.rearrange("p (j c) -> p j c", c=128)
                        i3 = IDENTB.ap.unsqueeze(1).broadcast_to([128, 31, 128])
                        w3 = CST.ap[:, C_CW + i * 31:C_CW + i * 31 + 31].unsqueeze(2).broadcast_to([128, 31, 128])
                        P.add("dve", lambda e, o=o3, x=i3, y=w3: e.tensor_tensor(out=o, in0=x, in1=y, op=ALU.mult),
                              rd=[IDENTB, CST], wr=[dv])
                    else:
                        for j in range(31):
                            ts(dv.c(j * 128, (j + 1) * 128), IDENTB, cst(C_CW + i * 31 + j), None, ALU.mult, extra_rd=[CST])
                    dsets.append(dv)
            zb = [(zmm(va, aa, jj), zmm(vg_, ag_, jj)) for jj in range(2)]
            for jj in range(2):
                i = 2 * q + jj
                ba, bz = zb[jj]
                cbv = Af(CB0 + (cnt["cb"] % 3) * PW, PW)
                cbb = Ab(CBB0 + (cnt["cb"] % 2) * 272, PW)
                cnt["cb"] += 1
                for sg in segs:
                    pd, t0, ln = sg["pd"], sg["t0"], sg["len"]
                    act(cbv.c(pd, pd + ln), PS(bz, t0, t0 + ln), AF.Sigmoid)
                    tt(cbv.c(pd, pd + ln), cbv.c(pd, pd + ln), PS(ba, t0, t0 + ln), ALU.mult)
                    if sg["kind"] == "pr":
                        cp(cbv.c(pd - 32, pd), Af(HC0 + l * 192 + i * 32, 32))
                    elif sg["kind"] == "wu":
                        memset(cbv.c(pd - 32, pd), 0.0)
                    elif sg["kind"] == "pr0":
                        ts(cbv.c(pd - 32, pd), cbv.c(pd - 64, pd - 32), FLAG.ap[:, 0:1], None, ALU.mult, extra_rd=[FLAG])
                    else:
                        cp(cbv.c(pd - 32, pd), Af(STC0 + (i * 2 + sg["s"]) * 32, 32))
                for sg in segs:
                    pd, ln = sg["pd"], sg["len"]
                    src = cbv.c(pd + ln - 32, pd + ln)
                    if sg["kind"] in ("pr", "pr0"):
                        cp(Af(HC0 + l * 192 + i * 32, 32), src)
                    elif sg["kind"] == "wu":
                        if not LAY2:
                            ts(Af(HC0 + l * 192 + i * 32, 32), src, FLAG.ap[:, 0:1], None, ALU.mult, extra_rd=[FLAG])
                    else:
                        dma("sp", DR(ncsT[l, sg["s"], i * 128:(i + 1) * 128, :], ("ncs", l, sg["s"], i)), src)
                acc = ACCc(i)
                a512 = acc.c(0, 512)
                if PECONV:
                    act(cbb, cbv, AF.Copy)
                    bk = newbank()
                    dv = dsets[jj]
                    cc = C0[0]
                    mm_group([PS(bk, cc, 512)],
                             [(0, dv.ap[:, j * 128:(j + 1) * 128], cbb.ap[:, 2 + j + cc:514 + j]) for j in range(31)],
                             rd=[dv, cbb])
                    act(a512.c(cc, 512), PS(bk, cc, 512), AF.Identity, bias=cst(C_CB + i), extra_rd=[CST])
                else:
                    accb = Af(PA0, 512)
                    for j in range(16):
                        if j == 0:
                            ts(a512, cbv.c(2, 514), cst(C_CW + i * 31), cst(C_CB + i), ALU.mult, ALU.add, extra_rd=[CST])
                        else:
                            stt(a512, cbv.c(2 + j, 514 + j), cst(C_CW + i * 31 + j), a512, ALU.mult, ALU.add, extra_rd=[CST])
                        jb = 16 + j
                        if jb == 16:
                            ts(accb, cbv.c(2 + jb, 514 + jb), cst(C_CW + i * 31 + jb), None, ALU.mult, extra_rd=[CST])
                        elif jb < 31:
                            stt(accb, cbv.c(2 + jb, 514 + jb), cst(C_CW + i * 31 + jb), accb, ALU.mult, ALU.add, extra_rd=[CST])
                    tt(a512, a512, accb, ALU.add)

        bs_ = newstat()
        bq_ = newstat()
        for i in range(6):
            a512 = ACCc(i).c(0, 512)
            s = SQf(cnt["sq"], 512)
            cnt["sq"] += 1
            act(s, a512, AF.Square)
            mm_group([PS(bq_, 0, 512)], [(0, ONESB.ap, s.ap)], rd=[ONESB, s], first=(i == 0), last=(i == 5))
            s2 = SQf(cnt["sq"], 512)
            cnt["sq"] += 1
            act(s2, a512, AF.Copy)
            mm_group([PS(bs_, 0, 512)], [(0, ONESB.ap, s2.ap)], rd=[ONESB, s2], first=(i == 0), last=(i == 5))
        mu = Af(MU0, 512)
        var = Af(MU0 + 512, 512)
        rr = RSf(cnt["rs"], 512)
        cnt["rs"] += 1
        ts(mu, PS(bs_, 0, 512), 1.0 / 768, None, ALU.mult)
        tt(var, mu, mu, ALU.mult)
        stt(var, PS(bq_, 0, 512), 1.0 / 768, var, ALU.mult, ALU.subtract)
        act(rr, var, AF.Sqrt, scale=1.0, bias=EPSC.ap[:, 0:1], extra_rd=[EPSC])
        recip(rr, rr)
        for i in range(6):
            tt(ACCc(i).c(0, 512), ACCc(i).c(0, 512), mu, ALU.subtract)
        for i in range(6):
            tt(ACCc(i).c(0, 512), ACCc(i).c(0, 512), rr, ALU.mult)
        for i in range(6):
            a512 = ACCc(i).c(0, 512)
            for sg in segs:
                a0 = sg["pd"] - 32
                t0, ln = sg["t0"], sg["len"]
                act(MIXf(10 + i, n).c(t0, t0 + ln), a512.c(a0, a0 + ln), AF.Silu,
                    scale=cst(C_LG + i), bias=cst(C_LB + i), extra_rd=[CST])

        st = Stat(n, D)
        woutn = wres("wout%d" % l, D, 256)
        mixall = [MIXc(k, n) for k in range(NKC)]
        for mg in range(NKC // 2):
            vw, aw = wload(woutS[l][:, mg * 256:(mg + 1) * 256], NKC, woutn)
            b = [newbank(), newbank()]
            outs = [PSn(b[0], n), PSn(b[1], n)]
            pairs = []
            for k in range(NKC):
                for mm in range(2):
                    pairs.append((mm, aw[:, k, mm * 128:(mm + 1) * 128], mixall[k].ap))
            mm_group(outs, pairs, rd=[vw] + mixall)
            for mm in range(2):
                m = 2 * mg + mm
                st.add(outs[mm], m == NKC - 1)
                cp(Dc(m, n), outs[mm])
        postnorm(st, 3, n, 1.0, pk)

    def load_consts(l, tile):
        dma("sp", Af(CST0, NCST), DR(cst_d[l], "cst_d"))
        if CASTALL:
            dma("pool", V(Ab(PWB0, 512).ap.rearrange("p (g d) -> p g d", d=128), Ab(PWB0, 512).res),
                DR(pw_d[l].rearrange("g c d -> c g d"), "pw_src"))
        else:
            dma("sp", V(Ab(PWB0, 512).ap.rearrange("p (g d) -> p g d", d=128), Ab(PWB0, 512).res),
                DR(pwb_s[l].rearrange("g c d -> c g d"), ("pw%d" % l, 0)))
        swt = Af(SWT0, 768)
        dma("sp", swt, DR(swT_d[l].rearrange("k h q -> k (h q)"), "swT_d"))
        dma("sp", Af(BR0, 768, 1), DR(sb_d[l], "sb_d"))
        wmt = Ab(WMT0, 768)
        for h in range(6):
            tt(wmt.c(h * 128, (h + 1) * 128), swt.c(h * 128, (h + 1) * 128), MASK, ALU.mult)
        if tile.get("samp"):
            sws = Af(SWS0, 384, 64)
            sws3 = sws.ap.rearrange("p (h q) -> p h q", q=64)
            dma("sp", V(sws3[0:32, :, 0:32], sws.res), DR(swT_d[l, 0:32, :, 0:32], "swT_d"))
            dma("sp", V(sws3[32:64, :, 32:64], sws.res), DR(swT_d[l, 0:32, :, 0:32], "swT_d"))
            wms = Ab(WMS0, 384, 64)
            for h in range(6):
                tt(wms.c(h * 64, (h + 1) * 64), sws.c(h * 64, (h + 1) * 64), V(MASK.ap[0:64, 0:64], MASK.res), ALU.mult)
            br2 = Af(BR20, 384, 1)
            for s_ in range(2):
                dma("sp", V(br2.ap.rearrange("p (h s t) -> p h s t", s=2, t=32)[:, :, s_, :], br2.res),
                    DR(sb_d[l].rearrange("o (h q) -> o h q", q=128)[:, :, 0:32], "sb_d"))
            for s_ in range(2):
                stc = Af(STC0, 384)
                d3 = stc.ap.rearrange("p (i s t) -> p i s t", s=2, t=32)
                dma("sp", V(d3[:, :, s_, 2:32], stc.res), DR(scT[l, s_].rearrange("(i p) t -> p i t", p=128), "scT"))
                stp = Af(STP0, 128)
                d3p = stp.ap.rearrange("p (g s t) -> p g s t", s=2, t=16)
                dma("sp", V(d3p[:, :, s_, 1:16], stp.res), DR(spT[l, s_].rearrange("(g p) t -> p g t", p=128), "spT"))

    tiles = []
    ch4 = [(0, 128), (128, 128), (256, 128), (384, 128)]
    if LAY2:
        tiles.append(dict(n=384, warm=True, first_pd=320,
                          segs=[dict(kind="wu", t0=0, len=256, pd=32), dict(kind="pr0", t0=256, len=128, pd=320)],
                          tcs=[(0, 128), (128, 128), (256, 128)],
                          xl=[(0, 384, "x", 0)], yst=[(256, 128, "y", 0)]))
        nmid = (TPC - 512) // 512
        for t in range(nmid):
            tiles.append(dict(n=512, segs=[dict(kind="pr", t0=0, len=512, pd=32)], tcs=ch4,
                              xl=[(0, 512, "x", 384 + 512 * t)], yst=[(0, 512, "y", 128 + 512 * t)]))
        tiles.append(dict(n=448, samp=True,
                          segs=[dict(kind="pr", t0=0, len=384, pd=32),
                                dict(kind="s", s=0, t0=WU, len=32, pd=32 + WU + 32),
                                dict(kind="s", s=1, t0=WU + 32, len=32, pd=32 + WU + 32 + 64)],
                          tcs=[(0, 128), (128, 128), (256, 128), (384, 64)],
                          xl=[(0, 384, "x", 384 + 512 * nmid), (384, 64, "xs", 0)],
                          yst=[(0, 384, "y", 128 + 512 * nmid), (384, 64, "ys", 0)]))
    else:
        tiles.append(dict(n=WU + 64, warm=True, samp=True,
                          segs=[dict(kind="wu", t0=0, len=WU, pd=32),
                                dict(kind="s", s=0, t0=WU, len=32, pd=32 + WU + 32),
                                dict(kind="s", s=1, t0=WU + 32, len=32, pd=32 + WU + 32 + 64)],
                          tcs=[(0, 128), (128, 128), (256, 128), (384, 64)],
                          xl=[(0, WU, "x", 0), (WU, 64, "xs", 0)], yst=[(WU, 64, "ys", 0)]))
        for t in range(NPT):
            tiles.append(dict(n=512, first_pd=(32 if t == 0 else None), segs=[dict(kind="pr", t0=0, len=512, pd=32)],
                              tcs=ch4, xl=[(0, 512, "x", WU + 512 * t)], yst=[(0, 512, "y", 512 * t)]))
    if cfg.get("tiles") is not None:
        tiles = [tiles[i] for i in cfg["tiles"]]

    for ti_, tile in enumerate(tiles):
        n = tile["n"]
        C0[0] = 0
        for k in range(NKC):
            for (xc, ncol, src, sc) in tile["xl"]:
                srcap = xT if src == "x" else xsT
                dma("sp", Af(X0 + k * 512 + xc, ncol), DR(srcap[k * 128:(k + 1) * 128, sc:sc + ncol], src + "T"))
        for l in range(NL):
            C0[0] = 0
            tile_l = tile
            if tile.get("warm") and cfg.get("shrink", True):
                dd = NL - 1 - l
                tl = dict(tile)
                if LAY2:
                    c0 = 224 if dd == 0 else (128 if dd == 1 else 0)
                    tl["segs"] = [dict(kind="wu", t0=c0, len=256 - c0, pd=32 + c0)] + tile["segs"][1:]
                    tl["tcs"] = [(t_, 128) for t_ in range((c0 + 127) // 128 * 128, 384, 128)]
                else:
                    c0 = 352 if dd == 0 else (256 if dd == 1 else (128 if dd <= 3 else 0))
                    tl["segs"] = [dict(kind="wu", t0=c0, len=WU - c0, pd=32 + c0)] + tile["segs"][1:]
                    tl["tcs"] = [(t_, 128) for t_ in range((c0 + 127) // 128 * 128, WU, 128)] + [(WU, 64)]
                C0[0] = c0
                tile_l = tl
            load_consts(l, tile)
            stg = cfg.get("stages", ("f0", "mx", "f1"))
            WQ["q"] = "pool" if (tile.get("warm") or CASTALL) else "sp"
            if "f0" in stg:
                stage_begin(l * 3, 68)
                ffn(l, 0, n, False)
                stage_end()
            if "mx" in stg:
                stage_begin(l * 3 + 1, 22)
                mixer(l, tile_l)
                stage_end()
            if "f1" in stg:
                stage_begin(l * 3 + 2, 68)
                ffn(l, 1, n, False)
                stage_end()
        C0[0] = 0
        for k in range(NKC):
            for (xc, ncol, dst, dc) in tile["yst"]:
                dstap = yT if dst == "y" else ysT
                dma("sp", DR(dstap[k * 128:(k + 1) * 128, dc:dc + ncol], (dst, k, ti_)), Af(X0 + k * 512 + xc, ncol))
    for l in range(NL):
        dma("sp", DR(nppT[l].rearrange("(g p) t -> p g t", p=128), ("npp", l)),
            V(Af(HP0 + l * 64, 64).ap.rearrange("p (g t) -> p g t", t=16), Af(HP0 + l * 64, 64).res))
        dma("sp", DR(ncpT[l].rearrange("(i p) t -> p i t", p=128), ("ncp", l)),
            V(Af(HC0 + l * 192, 192).ap.rearrange("p (i t) -> p i t", t=32), Af(HC0 + l * 192, 192).res))

    P.emit(nc, es)
    es.close()
    return nc


def _prep_common(inp, NL):
    f = np.float32
    ng = np.asarray(inp["norm_g"], f)[:NL]
    cst = np.zeros((NL, 128, NCST), f)
    cst[:, :, 0:96] = ng.reshape(NL, 6, 16, 128).transpose(0, 3, 1, 2).reshape(NL, 128, 96)
    cst[:, :, 96:100] = np.asarray(inp["pool_scale"], f)[:NL].reshape(NL, 4, 128).transpose(0, 2, 1)
    cst[:, :, 100:106] = np.asarray(inp["sgu_norm_g"], f)[:NL].reshape(NL, 6, 128).transpose(0, 2, 1)
    cw = np.asarray(inp["conv_w"], f)[:NL]
    cst[:, :, 106:292] = cw.reshape(NL, 31, 6, 128).transpose(0, 3, 2, 1).reshape(NL, 128, 186)
    cst[:, :, 292:298] = np.asarray(inp["conv_b"], f)[:NL].reshape(NL, 6, 128).transpose(0, 2, 1)
    cst[:, :, 298:304] = np.asarray(inp["conv_ln_g"], f)[:NL].reshape(NL, 6, 128).transpose(0, 2, 1)
    cst[:, :, 304:310] = np.asarray(inp["conv_ln_b"], f)[:NL].reshape(NL, 6, 128).transpose(0, 2, 1)
    sw = np.asarray(inp["sgu_w"], f)[:NL]
    swT = np.ascontiguousarray(sw.transpose(0, 3, 1, 2))
    sb = np.ascontiguousarray(np.asarray(inp["sgu_b"], f)[:NL].reshape(NL, 1, 768))
    k = np.arange(128)
    mask = (k[:, None] <= k[None, :]).astype(f)
    ident = np.eye(128, dtype=f)
    return dict(cst=cst, swT=swT, sb=sb, mask=mask, ident=ident,
                wg=np.ascontiguousarray(np.asarray(inp["ffn_w_gate"], f)[:NL]),
                wu=np.ascontiguousarray(np.asarray(inp["ffn_w_up"], f)[:NL]),
                wd=np.ascontiguousarray(np.asarray(inp["ffn_w_down"], f)[:NL]),
                win=np.ascontiguousarray(np.asarray(inp["w_in"], f)[:NL]),
                wout=np.ascontiguousarray(np.asarray(inp["w_out"], f)[:NL]),
                pw=np.ascontiguousarray(np.asarray(inp["pool_w"], f)[:NL]))


def run(inp, cfg):
    NL, TPC, NCO = cfg["NL"], cfg["TPC"], cfg["NCORES"]
    f = np.float32
    common = _prep_common(inp, NL)
    xp = np.asarray(inp["x_prompt"], f)[0]
    xs = np.asarray(inp["x_sample"], f)
    stp = np.asarray(inp["state_pool"], f)[:NL]
    stc = np.asarray(inp["state_conv"], f)[:NL]
    in_maps = []
    for c in range(NCO):
        m = dict(common)
        WUR = 256 if cfg.get("lay2", True) else WU
        lo = c * TPC - WUR
        xin = np.zeros((WUR + TPC, D), f)
        if c == 0:
            xin[WUR:] = xp[0:TPC]
        else:
            xin[:] = xp[lo:lo + WUR + TPC]
        m["xT"] = np.ascontiguousarray(xin.T)
        m["xsT"] = np.ascontiguousarray(xs[2 * c:2 * c + 2].reshape(64, D).T)
        m["spT"] = np.ascontiguousarray(stp[:, 2 * c:2 * c + 2].transpose(0, 1, 3, 2))
        m["scT"] = np.ascontiguousarray(stc[:, 2 * c:2 * c + 2].transpose(0, 1, 3, 2))
        m["flag"] = np.full((128, 1), 0.0 if c == 0 else 1.0, f)
        pos = np.arange(16) + c * TPC
        inv = np.zeros((128, 64), f)
        for g, w in enumerate(POOL_W):
            inv[:, g * 16:(g + 1) * 16] = (1.0 / np.minimum(w, pos + 1).astype(f))[None, :]
        m["invc"] = inv
        in_maps.append(m)
    nc = build(cfg)
    res = run_bass_kernel_spmd(nc, in_maps, core_ids=list(range(NCO)))
    R = res.results
    y_prompt = np.concatenate([r["yT"].T for r in R], axis=0)[None].astype(f)
    y_sample = np.concatenate([r["ysT"].T.reshape(2, 32, D) for r in R], axis=0).astype(f)
    last = R[NCO - 1]
    npp = np.ascontiguousarray(last["nppT"][:, :, 1:16].transpose(0, 2, 1))[:, None].astype(f)
    ncp = np.ascontiguousarray(last["ncpT"][:, :, 2:32].transpose(0, 2, 1))[:, None].astype(f)
    nps = np.concatenate([r["npsT"][:, :, :, 1:16].transpose(0, 1, 3, 2) for r in R], axis=1).astype(f)
    ncs = np.concatenate([r["ncsT"][:, :, :, 2:32].transpose(0, 1, 3, 2) for r in R], axis=1).astype(f)
    nv = np.concatenate([r["nvT"].transpose(0, 2, 1).reshape(NL, 2, 32, 768) for r in R], axis=1).astype(f)
    return (np.ascontiguousarray(y_prompt), np.ascontiguousarray(y_sample), npp, ncp,
            np.ascontiguousarray(nps), np.ascontiguousarray(ncs), np.ascontiguousarray(nv))


def kernel(**inputs):
    cfg = dict(NL=4, TPC=2048, NCORES=8)
    return run(inputs, cfg)
```

```python
import numpy as np
import concourse.bass as bass
import concourse.mybir as mybir
from concourse.bass_utils import run_bass_kernel_spmd
from contextlib import ExitStack

F32 = mybir.dt.float32
BF16 = mybir.dt.bfloat16
AF = mybir.ActivationFunctionType
ALU = mybir.AluOpType

D = 2048
DFF = 5632
DIN = 3584
NKC = 16
NHC = 44
WU = 384
EPS = 1e-6
GR = 64
PW = 544
NCST = 320
POOL_W = (2, 4, 8, 16)


class V:
    __slots__ = ("ap", "res")

    def __init__(self, ap, res):
        self.ap = ap
        self.res = list(res)

    def c(self, a, b):
        return V(self.ap[:, a:b], self.res)

    def p(self, a, b):
        return V(self.ap[a:b], self.res)


class Op:
    __slots__ = ("eng", "fn", "idx", "deps", "sig", "semval", "dma", "slot", "dval")

    def __init__(self, eng, fn, dma):
        self.eng = eng
        self.fn = fn
        self.dma = dma
        self.deps = []
        self.sig = False
        self.semval = 0
        self.slot = -1
        self.dval = 0


class Prog:
    ENGS = ("pe", "act", "dve", "pool", "sp")
    NDS = {"sp": 24, "pool": 8}

    def __init__(self):
        self.ops = {e: [] for e in self.ENGS}
        self.wr_c = {}
        self.wr_d = {}
        self.rd_c = {}
        self.rd_d = {}
        self.ndma = {"sp": 0, "pool": 0}
        self.dma_ops = {"sp": [], "pool": []}

    def add(self, eng, fn, rd=(), wr=(), dma=False):
        op = Op(eng, fn, dma)
        lst = self.ops[eng]
        op.idx = len(lst)
        lst.append(op)
        deps = {}
        rres = set()
        for v in rd:
            rres.update(v.res)
        wres = set()
        for v in wr:
            wres.update(v.res)
        psr = [r for r in rres if isinstance(r, tuple) and r[0] == "ps"]
        for r in psr:
            rres.discard(r)
            wres.add(r)

        def dep(o):
            deps[id(o)] = o

        for r in rres:
            w = self.wr_c.get(r)
            if w:
                for o in w.values():
                    dep(o)
            w = self.wr_d.get(r)
            if w is not None:
                dep(w)
        for r in wres:
            w = self.wr_c.get(r)
            if w:
                for o in w.values():
                    dep(o)
            w = self.wr_d.get(r)
            if w is not None:
                dep(w)
            w = self.rd_c.get(r)
            if w:
                for o in w.values():
                    dep(o)
            w = self.rd_d.get(r)
            if w:
                for o in w:
                    dep(o)
        for o in deps.values():
            if o.dma:
                op.deps.append(o)
            elif o.eng == eng and not dma:
                if eng == "pe":
                    continue
                if op.idx - o.idx > 1:
                    continue
                o.sig = True
                op.deps.append(o)
            else:
                o.sig = True
                op.deps.append(o)
        if dma:
            q = eng
            j = self.ndma[q]
            self.ndma[q] = j + 1
            op.slot = j % self.NDS[q]
            op.dval = 16 * (j // self.NDS[q] + 1)
            self.dma_ops[q].append(op)
            for r in rres:
                self.rd_d.setdefault(r, []).append(op)
            for r in wres:
                self.wr_d[r] = op
                self.wr_c.pop(r, None)
                self.rd_c.pop(r, None)
                self.rd_d.pop(r, None)
        else:
            for r in rres:
                self.rd_c.setdefault(r, {})[eng] = op
            for r in wres:
                self.wr_c[r] = {eng: op}
                self.wr_d.pop(r, None)
                self.rd_c.pop(r, None)
                self.rd_d.pop(r, None)
        return op

    def emit(self, nc, es):
        sems = {e: es.enter_context(nc.semaphore("s_" + e)) for e in ("pe", "act", "dve", "pool")}
        dsem = {q: [es.enter_context(nc.semaphore("d_%s%d" % (q, i))) for i in range(n)]
                for q, n in self.NDS.items()}
        for e in ("pe", "act", "dve", "pool"):
            cnt = 0
            for op in self.ops[e]:
                if op.sig and not op.dma:
                    cnt += 1
                    op.semval = cnt
        block = es.enter_context(nc.Block())
        prog = self

        def run(ename, eng):
            waited = {}

            def wait(key, sem, val):
                if waited.get(key, 0) >= val:
                    return
                waited[key] = val
                eng.wait_ge(sem, val)

            for op in prog.ops[ename]:
                for o in op.deps:
                    if o.dma:
                        wait(("d", o.eng, o.slot), dsem[o.eng][o.slot], o.dval)
                    else:
                        wait(("c", o.eng), sems[o.eng], o.semval)
                if op.dma:
                    if op.dval > 16:
                        wait(("d", ename, op.slot), dsem[ename][op.slot], op.dval - 16)
                    inst = op.fn(eng)
                    inst.then_inc(dsem[ename][op.slot], 16)
                else:
                    inst = op.fn(eng)
                    if op.sig:
                        inst.then_inc(sems[ename], 1)
            if ename == "sp":
                for q in ("sp", "pool"):
                    last = {}
                    for o in prog.dma_ops[q]:
                        last[o.slot] = o.dval
                    for s, v in last.items():
                        eng.wait_ge(dsem[q][s], v)

        @block.tensor
        def _(e):
            run("pe", e)

        @block.scalar
        def _(e):
            run("act", e)

        @block.vector
        def _(e):
            run("dve", e)

        @block.gpsimd
        def _(e):
            run("pool", e)

        @block.sync
        def _(e):
            run("sp", e)


def build(cfg):
    NL = cfg["NL"]
    TPC = cfg["TPC"]
    NPT = TPC // 512
    CASTALL = cfg.get("castall", True)
    PECONV = cfg.get("peconv", True)
    nc = bass.Bass("TRN2", target_bir_lowering=False)
    P = Prog()

    def din(name, shape, dt=F32):
        return nc.dram_tensor(name, list(shape), dt, kind="ExternalInput").ap()

    def dout(name, shape):
        return nc.dram_tensor(name, list(shape), F32, kind="ExternalOutput").ap()

    def dscr(name, shape, dt=BF16):
        return nc.dram_tensor(name, list(shape), dt, kind="Internal").ap()

    LAY2 = cfg.get("lay2", True)
    WUR = 256 if LAY2 else WU
    xT = din("xT", [D, WUR + TPC])
    xsT = din("xsT", [D, 64])
    spT = din("spT", [NL, 2, 512, 15])
    scT = din("scT", [NL, 2, 768, 30])
    flag_d = din("flag", [128, 1])
    invc_d = din("invc", [128, 64])
    cst_d = din("cst", [NL, 128, NCST])
    swT_d = din("swT", [NL, 128, 6, 128])
    sb_d = din("sb", [NL, 1, 768])
    mask_d = din("mask", [128, 128])
    identb_d = din("ident", [128, 128])
    wg_d = din("wg", [NL, 2, D, DFF])
    wu_d = din("wu", [NL, 2, D, DFF])
    wd_d = din("wd", [NL, 2, DFF, D])
    win_d = din("win", [NL, D, DIN])
    wout_d = din("wout", [NL, D, D])
    pw_d = din("pw", [NL, 4, 128, 128])

    yT = dout("yT", [D, TPC])
    ysT = dout("ysT", [D, 64])
    nppT = dout("nppT", [NL, 512, 16])
    ncpT = dout("ncpT", [NL, 768, 32])
    npsT = dout("npsT", [NL, 2, 512, 16])
    ncsT = dout("ncsT", [NL, 2, 768, 32])
    nvT = dout("nvT", [NL, 768, 64])

    wgb = dscr("wgb", [NL, 2, D, DFF])
    wub = dscr("wub", [NL, 2, D, DFF])
    wdb = dscr("wdb", [NL, 2, DFF, D])
    winb = dscr("winb", [NL, D, DIN])
    woutb = dscr("woutb", [NL, D, D])
    pwb_s = dscr("pwb", [NL, 4, 128, 128])

    es = ExitStack()
    cols = [0]

    def alloc(n):
        n = (n + GR - 1) // GR * GR
        c0 = cols[0]
        cols[0] += n
        return c0

    X0 = alloc(NKC * 512)
    HD0 = alloc(NKC * 512)
    R0 = alloc(NHC * 256)
    PB0 = alloc(4 * PW)
    PA0 = alloc(2 * PW)
    T20 = alloc(2 * 512)
    CB0 = alloc(3 * PW)
    CBB0 = alloc(2 * 272)
    VF0 = alloc(6 * 64)
    HP0 = alloc(NL * 64)
    HC0 = alloc(NL * 192)
    SQ0 = alloc(3 * 256)
    RS0 = alloc(2 * 512)
    MU0 = alloc(2 * 512)
    VT0 = alloc(4 * 384)
    CST0 = alloc(NCST)
    PWB0 = alloc(256)
    SWT0 = alloc(768)
    WMT0 = alloc(384)
    SWS0 = alloc(384)
    WMS0 = alloc(192)
    BR0 = alloc(768)
    BR20 = alloc(384)
    STC0 = alloc(384)
    STP0 = alloc(128)
    ONB0 = alloc(64)
    IDB0 = alloc(64)
    MSK0 = alloc(128)
    ONR0 = alloc(128)
    FLG0 = alloc(64)
    INV0 = alloc(64)
    IDF0 = alloc(128)
    EPS0 = alloc(64)
    NWS = cfg.get("NWS", 4)
    WR0 = alloc(NWS * 2048)
    TOT = cols[0]
    assert TOT * 4 <= 207 * 1024, TOT * 4
    print("arena bytes/partition", TOT * 4)
    arena_t = es.enter_context(nc.sbuf_tensor("arena", [128, TOT], F32))
    arena = arena_t[:]
    pst = [es.enter_context(nc.psum_tensor("ps%d" % i, [128, 512], F32)) for i in range(8)]

    def gres(c0, n):
        return range(c0 // GR, (c0 + n - 1) // GR + 1)

    def Af(c0, n, p1=128):
        return V(arena[0:p1, c0:c0 + n], gres(c0, n))

    def Ab(c0, n, p1=128):
        nc_ = (n + 1) // 2
        return V(arena[0:p1, c0:c0 + nc_].bitcast(BF16), gres(c0, nc_))

    def PS(b, a=0, e=512, p1=128):
        return V(pst[b][0:p1, a:e], [("ps", b)])

    def DR(ap, *names):
        return V(ap, names)

    C0 = [0]

    def Xc(k, n):
        return Af(X0 + k * 512, n).c(C0[0], n)

    def Dc(m, n):
        return Af(HD0 + m * 512, n).c(C0[0], n)

    def Hc(k, n):
        return Ab(HD0 + k * 512, n).c(C0[0], n)

    def HIDc(j, n):
        return Ab(R0 + j * 256, n).c(C0[0], n)

    def MIXc(i, n):
        return Ab(R0 + i * 256, n).c(C0[0], n)

    def MIXf(i, n):
        return Ab(R0 + i * 256, n)

    def PSn(b, n):
        return PS(b, C0[0], n)

    ACC0 = R0 + 16 * 256
    GV0 = ACC0 + 6 * PW
    assert GV0 + 6 * 512 <= R0 + NHC * 256

    def ACCc(i):
        return Af(ACC0 + i * PW, PW)

    def GVc(i, n):
        return Af(GV0 + i * 512, n).c(C0[0], n)

    def VFMc(i, n):
        return Ab(HD0 + i * 512 + 256, n).c(C0[0], n)

    def POOLEDc(g, n):
        return Ab(HD0 + (6 + g) * 512 + 256, n)

    def VTMc(tc, p1=128):
        return Ab(VT0 + tc * 384, 768, p1)

    def T2(i, n):
        return Af(T20 + (i % 2) * 512, n).c(C0[0], n)

    def SQ(i, n):
        return Ab(SQ0 + (i % 3) * 256, n).c(C0[0], n)

    def RS(i, n):
        return Af(RS0 + (i % 2) * 512, n).c(C0[0], n)

    def SQf(i, n):
        return Ab(SQ0 + (i % 3) * 256, n)

    def RSf(i, n):
        return Af(RS0 + (i % 2) * 512, n)

    ONESB = Ab(ONB0, 128)
    IDENTB = Ab(IDB0, 128)
    MASK = Af(MSK0, 128)
    ONEROW = Af(ONR0, 128, 1)
    FLAG = Af(FLG0, 1)
    INVC = Af(INV0, 64)
    EPSC = Af(EPS0, 1)

    cnt = {"bank": 0, "stat": 0, "sq": 0, "rs": 0, "t2": 0, "w": 0, "cb": 0}

    def newbank():
        b = cnt["bank"] % 6
        cnt["bank"] += 1
        return b

    def newstat():
        b = 6 + cnt["stat"] % 2
        cnt["stat"] += 1
        return b

    def dma(q, out, in_):
        P.add(q, lambda e, o=out.ap, i=in_.ap: e.dma_start(out=o, in_=i), rd=[in_], wr=[out], dma=True)

    def act(out, in_, func, scale=None, bias=None, extra_rd=()):
        kw = {}
        if scale is not None:
            kw["scale"] = scale
        if bias is not None:
            kw["bias"] = bias
        P.add("act", lambda e, o=out.ap, i=in_.ap, f=func, kw=kw: e.activation(out=o, in_=i, func=f, **kw),
              rd=[in_] + list(extra_rd), wr=[out])

    def tt(out, a, b, op, eng="dve"):
        P.add(eng, lambda e, o=out.ap, x=a.ap, y=b.ap, op=op: e.tensor_tensor(out=o, in0=x, in1=y, op=op),
              rd=[a, b], wr=[out])

    def ts(out, a, s1, s2, op0, op1=None, extra_rd=(), eng="dve"):
        def fn(e, o=out.ap, x=a.ap):
            if op1 is None:
                return e.tensor_scalar(out=o, in0=x, scalar1=s1, scalar2=None, op0=op0)
            return e.tensor_scalar(out=o, in0=x, scalar1=s1, scalar2=s2, op0=op0, op1=op1)
        P.add(eng, fn, rd=[a] + list(extra_rd), wr=[out])

    def stt(out, a, s, b, op0, op1, extra_rd=(), eng="dve"):
        P.add(eng, lambda e, o=out.ap, x=a.ap, y=b.ap: e.scalar_tensor_tensor(out=o, in0=x, scalar=s, in1=y, op0=op0, op1=op1),
              rd=[a, b] + list(extra_rd), wr=[out])

    def cp(out, in_, eng="dve"):
        P.add(eng, lambda e, o=out.ap, i=in_.ap: e.tensor_copy(out=o, in_=i), rd=[in_], wr=[out])

    def recip(out, in_):
        P.add("dve", lambda e, o=out.ap, i=in_.ap: e.reciprocal(out=o, in_=i), rd=[in_], wr=[out])

    def memset(out, val, eng="dve"):
        P.add(eng, lambda e, o=out.ap: e.memset(o, val), rd=[], wr=[out])

    def mm_group(outs, pairs, rd, first=True, last=True):
        seen = set()
        plan = []
        lastidx = {}
        for n, (oi, l, r) in enumerate(pairs):
            lastidx[oi] = n
        for n, (oi, l, r) in enumerate(pairs):
            st = first and (oi not in seen)
            seen.add(oi)
            sp = last and lastidx[oi] == n
            plan.append((outs[oi].ap, l, r, st, sp))

        def fn(e):
            inst = None
            for (o, l, r, st, sp) in plan:
                inst = e.matmul(o, lhsT=l, rhs=r, start=st, stop=sp)
            return inst
        P.add("pe", fn, rd=rd, wr=outs)

    memset(Af(PB0, 4 * PW), 0.0)
    memset(Af(PA0, 2 * PW), 0.0)
    memset(Af(CB0, 3 * PW), 0.0)
    memset(Af(SWS0, 384), 0.0)
    memset(Af(HP0, NL * 64), 0.0)
    memset(Af(HC0, NL * 192), 0.0)
    memset(Af(R0, NHC * 256), 0.0)
    memset(Af(STC0, 384), 0.0)
    memset(Af(STP0, 128), 0.0)
    memset(ONESB, 1.0)
    memset(ONEROW, 1.0)
    memset(EPSC, EPS)
    dma("sp", MASK, DR(mask_d[:, :], "mask_d"))
    dma("sp", Af(IDF0, 128), DR(identb_d[:, :], "ident_d"))
    dma("sp", FLAG, DR(flag_d[:, :], "flag_d"))
    dma("sp", INVC, DR(invc_d[:, :], "invc_d"))
    cp(IDENTB, Af(IDF0, 128))

    def conv_mat(src, dst, name, nrows, rstep):
        for r0 in range(0, nrows, rstep):
            r1 = min(nrows, r0 + rstep)
            dma("pool", DR(dst[r0:r1, :], (name, r0 // rstep)), DR(src[r0:r1, :], name + "_src"))

    def wres(name, nrows, rstep, r0=0, r1=None):
        r1 = nrows if r1 is None else r1
        return [(name, i) for i in range(r0 // rstep, (r1 - 1) // rstep + 1)]

    convlist = {}

    def conv_list(src, dst, name, nrows, rstep):
        out = []
        for r0 in range(0, nrows, rstep):
            r1 = min(nrows, r0 + rstep)
            out.append((DR(dst[r0:r1, :], (name, r0 // rstep)), DR(src[r0:r1, :], name + "_src")))
        return out

    for l in range(NL):
        for f in range(2):
            convlist[l * 3 + 2 * f] = (conv_list(wg_d[l, f], wgb[l, f], "wg%d_%d" % (l, f), D, 128)
                                       + conv_list(wu_d[l, f], wub[l, f], "wu%d_%d" % (l, f), D, 128)
                                       + conv_list(wd_d[l, f], wdb[l, f], "wd%d_%d" % (l, f), DFF, 256))
        convlist[l * 3 + 1] = (conv_list(win_d[l], winb[l], "win%d" % l, D, 128)
                               + [(DR(pwb_s[l].rearrange("g c d -> (g c) d"), ("pw%d" % l, 0)),
                                   DR(pw_d[l].rearrange("g c d -> (g c) d"), "pw_src"))]
                               + conv_list(wout_d[l], woutb[l], "wout%d" % l, D, 256))
    if CASTALL:
        convlist = {}
        wgS, wuS, wdS, winS, woutS = wg_d, wu_d, wd_d, win_d, wout_d
    else:
        wgS, wuS, wdS, winS, woutS = wgb, wub, wdb, winb, woutb
    WQ = {"q": "pool", "pend": [], "left": 0}

    def conv_emit(k):
        for _ in range(k):
            if not WQ["pend"]:
                return
            o, i = WQ["pend"].pop(0)
            dma("pool", o, i)

    def stage_begin(sid, nloads):
        if WQ["q"] != "pool":
            return
        WQ["pend"].extend(convlist.get(sid + 2, []))
        WQ["left"] = nloads

    def stage_end():
        if WQ["q"] != "pool":
            return
        conv_emit(len(WQ["pend"]))

    for sid in (0, 1):
        WQ["pend"].extend(convlist.get(sid, []))
    conv_emit(len(WQ["pend"]))

    def wload(src_ap, nk, names):
        s = cnt["w"] % NWS
        cnt["w"] += 1
        v = Ab(WR0 + s * 2048, nk * 256)
        ap3 = v.ap.rearrange("p (k n) -> p k n", n=256)
        dst = V(ap3, v.res)
        dma(WQ["q"], dst, DR(src_ap.rearrange("(k p) n -> p k n", p=128), *names))
        if WQ["q"] == "pool":
            left = max(1, WQ["left"])
            conv_emit((len(WQ["pend"]) + left - 1) // left)
            WQ["left"] -= 1
        return v, ap3

    class Stat:
        def __init__(self, n, dim):
            self.n = n
            self.bank = newstat()
            self.i = 0
            self.dim = dim

        def add(self, src, last):
            s = SQ(cnt["sq"], self.n)
            cnt["sq"] += 1
            act(s, src, AF.Square)
            mm_group([PSn(self.bank, self.n)], [(0, ONESB.ap, s.ap)], rd=[ONESB, s],
                     first=(self.i == 0), last=last)
            self.i += 1

        def finish(self):
            r = RS(cnt["rs"], self.n)
            cnt["rs"] += 1
            act(r, PSn(self.bank, self.n), AF.Sqrt, scale=1.0 / self.dim, bias=EPSC.ap[:, 0:1], extra_rd=[EPSC])
            recip(r, r)
            return r

    def prenorm(gcol, n, pk=False):
        st = Stat(n, D)
        for k in range(NKC):
            st.add(Xc(k, n), k == NKC - 1)
        plvl = cfg.get("plvl", 9)
        if plvl < 1:
            return
        r = st.finish()
        if plvl < 2:
            return
        CST = Af(CST0, NCST)
        for k in range(NKC):
            stt(Hc(k, n), Xc(k, n), CST.ap[:, gcol * 16 + k:gcol * 16 + k + 1], r, ALU.mult, ALU.mult, extra_rd=[CST],
                eng=("pool" if (pk and k % 3 == 2) else "dve"))

    def postnorm(st, gcol, n, half, pk=False):
        r = st.finish()
        CST = Af(CST0, NCST)
        for m in range(NKC):
            stt(Dc(m, n), Dc(m, n), CST.ap[:, gcol * 16 + m:gcol * 16 + m + 1], r, ALU.mult, ALU.mult, extra_rd=[CST])
        for m in range(NKC):
            stt(Xc(m, n), Dc(m, n), half, Xc(m, n), ALU.mult, ALU.add)

    def ffn(l, f, n, pk=False):
        lvl = cfg.get("lvl", 9)
        prenorm(0 if f == 0 else 4, n, pk)
        if lvl < 2:
            return
        wgn = wres("wg%d_%d" % (l, f), D, 128)
        wun = wres("wu%d_%d" % (l, f), D, 128)
        hall = [Hc(k, n) for k in range(NKC)]
        for jg in range(NHC // 2):
            vg, ag = wload(wgS[l, f][:, jg * 256:(jg + 1) * 256], NKC, wgn)
            vu, au = wload(wuS[l, f][:, jg * 256:(jg + 1) * 256], NKC, wun)
            kb = None
            if jg == 0 and cfg.get("kouter", True):
                kb = [newbank() for _ in range(4)]
                kouts = [PSn(b_, n) for b_ in kb]
                for k in range(NKC):
                    mm_group(kouts, [(0, ag[:, k, 0:128], hall[k].ap), (1, au[:, k, 0:128], hall[k].ap),
                                     (2, ag[:, k, 128:256], hall[k].ap), (3, au[:, k, 128:256], hall[k].ap)],
                             rd=[vg, vu, hall[k]], first=(k == 0), last=(k == NKC - 1))
            for jj in range(2):
                j = 2 * jg + jj
                if kb is not None:
                    bg, bu = kb[2 * jj], kb[2 * jj + 1]
                else:
                    bg = newbank()
                    mm_group([PSn(bg, n)], [(0, ag[:, k, jj * 128:(jj + 1) * 128], hall[k].ap) for k in range(NKC)],
                             rd=[vg] + hall)
                    bu = newbank()
                    mm_group([PSn(bu, n)], [(0, au[:, k, jj * 128:(jj + 1) * 128], hall[k].ap) for k in range(NKC)],
                             rd=[vu] + hall)
                t = T2(cnt["t2"], n)
                cnt["t2"] += 1
                act(t, PSn(bg, n), AF.Silu)
                tt(HIDc(j, n), t, PSn(bu, n), ALU.mult)
        if lvl < 3:
            return
        st = Stat(n, D)
        parts = [(0, 16), (16, 32), (32, 44)]
        for mg in range(NKC // 2):
            b = [newbank(), newbank()]
            outs = [PSn(b[0], n), PSn(b[1], n)]
            for pi, (k0, k1) in enumerate(parts):
                vw, aw = wload(wdS[l, f][k0 * 128:k1 * 128, mg * 256:(mg + 1) * 256], k1 - k0,
                               wres("wd%d_%d" % (l, f), DFF, 256, k0 * 128, k1 * 128))
                hs = [HIDc(k, n) for k in range(k0, k1)]
                pairs = []
                for k in range(k0, k1):
                    for mm in range(2):
                        pairs.append((mm, aw[:, k - k0, mm * 128:(mm + 1) * 128], hs[k - k0].ap))
                mm_group(outs, pairs, rd=[vw] + hs, first=(pi == 0), last=(pi == 2))
            for mm in range(2):
                m = 2 * mg + mm
                if cfg.get("dk", 3) & 1:
                    st.add(outs[mm], m == NKC - 1)
                if cfg.get("dk", 3) & 2:
                    cp(Dc(m, n), outs[mm])
        if lvl < 4:
            return
        postnorm(st, 1 if f == 0 else 5, n, 0.5, pk)

    def mixer(l, tile):
        n = tile["n"]
        pk = False
        segs = tile["segs"]
        CST = Af(CST0, NCST)

        def cst(c):
            return CST.ap[:, c:c + 1]
        C_PSC, C_SGG, C_CW, C_CB, C_LG, C_LB = 96, 100, 106, 292, 298, 304
        prenorm(2, n, pk)
        hall = [Hc(k, n) for k in range(NKC)]
        winn = wres("win%d" % l, D, 128)

        def win_slot(s):
            return wload(winS[l][:, s * 256:(s + 1) * 256], NKC, winn)

        def zmm(vw, aw, jj):
            b = newbank()
            mm_group([PSn(b, n)], [(0, aw[:, k, jj * 128:(jj + 1) * 128], hall[k].ap) for k in range(NKC)],
                     rd=[vw] + hall)
            return b

        def seg_pd(sg):
            return sg["pd"]

        pbanks = []
        for q in range(2):
            vp, ap_ = win_slot(q)
            for jj in range(2):
                pbanks.append(zmm(vp, ap_, jj))
        pwn = [("pw%d" % l, 0)]
        pool_tail = []
        for g in range(4):
            pb = Af(PB0 + g * PW, PW)
            for sg in segs:
                pd, t0, ln = sg["pd"], sg["t0"], sg["len"]
                act(pb.c(pd, pd + ln), PS(pbanks[g], t0, t0 + ln), AF.Copy)
                if sg["kind"] == "pr":
                    cp(pb.c(pd - 16, pd), Af(HP0 + l * 64 + g * 16, 16))
                elif sg["kind"] == "wu":
                    memset(pb.c(pd - 16, pd), 0.0)
                elif sg["kind"] == "pr0":
                    ts(pb.c(pd - 16, pd), pb.c(pd - 48, pd - 32), FLAG.ap[:, 0:1], None, ALU.mult, extra_rd=[FLAG])
                else:
                    cp(pb.c(pd - 16, pd), Af(STP0 + (g * 2 + sg["s"]) * 16, 16))
            for sg in segs:
                pd, ln = sg["pd"], sg["len"]
                src = pb.c(pd + ln - 16, pd + ln)
                if sg["kind"] in ("pr", "pr0"):
                    cp(Af(HP0 + l * 64 + g * 16, 16), src)
                elif sg["kind"] == "wu":
                    if not LAY2:
                        ts(Af(HP0 + l * 64 + g * 16, 16), src, FLAG.ap[:, 0:1], None, ALU.mult, extra_rd=[FLAG])
                else:
                    dma("sp", DR(npsT[l, sg["s"], g * 128:(g + 1) * 128, :], ("nps", l, sg["s"], g)), src)
            bufs = [pb, Af(PA0, PW), Af(PA0 + PW, PW)]
            cur = 0
            sh = 1
            for step in range(g + 1):
                nxt = 1 if cur != 1 else 2
                lo = 2 * sh - 1
                tt(bufs[nxt].c(lo, PW), bufs[cur].c(lo, PW), bufs[cur].c(lo - sh, PW - sh), ALU.add)
                cur = nxt
                sh *= 2
            w = POOL_W[g]
            pooled = POOLEDc(g, 512)
            stt(pooled, bufs[cur].c(32, PW), 1.0 / w, pb.c(32, PW), ALU.mult, ALU.subtract)
            fp = tile.get("first_pd")
            if fp:
                tmp = Af(MU0, 16)
                tt(tmp, bufs[cur].c(fp, fp + 16), INVC.c(g * 16, g * 16 + 16), ALU.mult)
                tt(pooled.c(fp - 32, fp - 16), tmp, pb.c(fp, fp + 16), ALU.subtract)
            pool_tail.append((g, pooled))

        def pool_finish():
            for (g, pooled) in pool_tail:
                b = newbank()
                pwv = Ab(PWB0, 512)
                mm_group([PS(b, 0, 512)], [(0, pwv.ap[:, g * 128:(g + 1) * 128], pooled.ap)], rd=[pwv, pooled])
                for sg in segs:
                    a0 = sg["pd"] - 32
                    t0, ln = sg["t0"], sg["len"]
                    ts(MIXf(g, n).c(t0, t0 + ln), PS(b, a0, a0 + ln), cst(C_PSC + g), None, ALU.mult, extra_rd=[CST])

        st = Stat(n, 768)
        for q in range(3):
            vv, av = win_slot(5 + q)
            for jj in range(2):
                c_ = 2 * q + jj
                b = zmm(vv, av, jj)
                act(GVc(c_, n), PSn(b, n), AF.Gelu)
                st.add(GVc(c_, n), c_ == 5)
        r = st.finish()
        for c_ in range(6):
            stt(VFMc(c_, n), GVc(c_, n), cst(C_SGG + c_), r, ALU.mult, ALU.mult, extra_rd=[CST])
            if tile.get("samp"):
                vf = Af(VF0 + c_ * 64, 64)
                stt(vf, GVc(c_, n).c(WU - C0[0], WU + 64 - C0[0]), cst(C_SGG + c_), r.c(WU - C0[0], WU + 64 - C0[0]), ALU.mult, ALU.mult, extra_rd=[CST])
                dma("sp", DR(nvT[l, c_ * 128:(c_ + 1) * 128, :], ("nv", l, c_)), vf)
        tcs = tile["tcs"]
        for ti, (t0, ntok) in enumerate(tcs):
            b0 = newbank()
            b1 = newbank()
            outs = [PS(b0, 0, 512, ntok), PS(b1, 0, 256, ntok)]
            pairs = []
            for c_ in range(6):
                o = outs[0].ap[:, c_ * 128:(c_ + 1) * 128] if c_ < 4 else outs[1].ap[:, (c_ - 4) * 128:(c_ - 3) * 128]
                pairs.append((o, VFMc(c_, n).ap[:, t0 - C0[0]:t0 - C0[0] + ntok]))
            plan = list(pairs)

            def fn(e, plan=plan):
                inst = None
                for (o, l_) in plan:
                    inst = e.matmul(o, lhsT=l_, rhs=IDENTB.ap, start=True, stop=True)
                return inst
            P.add("pe", fn, rd=[IDENTB] + [VFMc(c_, n) for c_ in range(6)], wr=outs)
            vt = VTMc(ti, ntok)
            act(V(vt.ap[:, 0:512], vt.res), outs[0], AF.Copy)
            cp(V(vt.ap[:, 512:768], vt.res), outs[1])
        pool_finish()
        WMT = Ab(WMT0, 768)
        WMS = Ab(WMS0, 384, 64)
        BROW = Af(BR0, 768, 1)
        BROW2 = Af(BR20, 384, 1)
        for q in range(3):
            vz, az = win_slot(2 + q)
            for jj in range(2):
                h = 2 * q + jj
                bzu = zmm(vz, az, jj)
                u = T2(cnt["t2"], n)
                cnt["t2"] += 1
                act(u, PSn(bzu, n), AF.Gelu)
                bs = newbank()
                plan = []
                rds = [ONEROW, BROW, BROW2, WMT, WMS]
                for ti, (t0, ntok) in enumerate(tcs):
                    vt = VTMc(ti, ntok)
                    rds.append(vt)
                    if ntok == 128:
                        plan.append((pst[bs][:, t0:t0 + 128], ONEROW.ap, BROW.ap[:, h * 128:(h + 1) * 128], True, False))
                        plan.append((pst[bs][:, t0:t0 + 128], vt.ap[:, h * 128:(h + 1) * 128], WMT.ap[:, h * 128:(h + 1) * 128], False, True))
                    else:
                        plan.append((pst[bs][:, t0:t0 + 64], ONEROW.ap, BROW2.ap[:, h * 64:(h + 1) * 64], True, False))
                        plan.append((pst[bs][:, t0:t0 + 64], vt.ap[:, h * 128:(h + 1) * 128], WMS.ap[:, h * 64:(h + 1) * 64], False, True))

                def fn(e, plan=plan):
                    inst = None
                    for (o, l_, r_, st_, sp_) in plan:
                        inst = e.matmul(o, lhsT=l_, rhs=r_, start=st_, stop=sp_)
                    return inst
                P.add("pe", fn, rd=rds, wr=[PSn(bs, n)])
                tt(MIXc(4 + h, n), u, PSn(bs, n), ALU.mult)

        for q in range(3):
            va, aa = win_slot(8 + q)
            vg_, ag_ = win_slot(11 + q)
            dsets = []
            if PECONV:
                for jj in range(2):
                    i = 2 * q + jj
                    s_ = cnt["w"] % NWS
                    cnt["w"] += 1
                    dv = Ab(WR0 + s_ * 2048, 31 * 128)
                    if cfg.get("diag1", True):
                        o3 = dv.ap.rearrange("p (j c) -> p j c", c=128)
                        i3 = IDENTB.ap.unsqueeze(1).broadcast_to([128, 31, 128])
                        w3 = CST.ap[:, C_CW + i * 31:C_CW + i * 31 + 31].unsqueeze(2).broadcast_to([128, 31, 128])
                        P.add("dve", lambda e, o=o3, x=i3, y=w3: e.tensor_tensor(out=o, in0=x, in1=y, op=ALU.mult),
                              rd=[IDENTB, CST], wr=[dv])
                    else:
                        for j in range(31):
                            ts(dv.c(j * 128, (j + 1) * 128), IDENTB, cst(C_CW + i * 31 + j), None, ALU.mult, extra_rd=[CST])
                    dsets.append(dv)
            zb = [(zmm(va, aa, jj), zmm(vg_, ag_, jj)) for jj in range(2)]
            for jj in range(2):
                i = 2 * q + jj
                ba, bz = zb[jj]
                cbv = Af(CB0 + (cnt["cb"] % 3) * PW, PW)
                cbb = Ab(CBB0 + (cnt["cb"] % 2) * 272, PW)
                cnt["cb"] += 1
                for sg in segs:
                    pd, t0, ln = sg["pd"], sg["t0"], sg["len"]
                    act(cbv.c(pd, pd + ln), PS(bz, t0, t0 + ln), AF.Sigmoid)
                    tt(cbv.c(pd, pd + ln), cbv.c(pd, pd + ln), PS(ba, t0, t0 + ln), ALU.mult)
                    if sg["kind"] == "pr":
                        cp(cbv.c(pd - 32, pd), Af(HC0 + l * 192 + i * 32, 32))
                    elif sg["kind"] == "wu":
                        memset(cbv.c(pd - 32, pd), 0.0)
                    elif sg["kind"] == "pr0":
                        ts(cbv.c(pd - 32, pd), cbv.c(pd - 64, pd - 32), FLAG.ap[:, 0:1], None, ALU.mult, extra_rd=[FLAG])
                    else:
                        cp(cbv.c(pd - 32, pd), Af(STC0 + (i * 2 + sg["s"]) * 32, 32))
                for sg in segs:
                    pd, ln = sg["pd"], sg["len"]
                    src = cbv.c(pd + ln - 32, pd + ln)
                    if sg["kind"] in ("pr", "pr0"):
                        cp(Af(HC0 + l * 192 + i * 32, 32), src)
                    elif sg["kind"] == "wu":
                        if not LAY2:
                            ts(Af(HC0 + l * 192 + i * 32, 32), src, FLAG.ap[:, 0:1], None, ALU.mult, extra_rd=[FLAG])
                    else:
                        dma("sp", DR(ncsT[l, sg["s"], i * 128:(i + 1) * 128, :], ("ncs", l, sg["s"], i)), src)
                acc = ACCc(i)
                a512 = acc.c(0, 512)
                if PECONV:
                    act(cbb, cbv, AF.Copy)
                    bk = newbank()
                    dv = dsets[jj]
                    cc = C0[0]
                    mm_group([PS(bk, cc, 512)],
                             [(0, dv.ap[:, j * 128:(j + 1) * 128], cbb.ap[:, 2 + j + cc:514 + j]) for j in range(31)],
                             rd=[dv, cbb])
                    act(a512.c(cc, 512), PS(bk, cc, 512), AF.Identity, bias=cst(C_CB + i), extra_rd=[CST])
                else:
                    accb = Af(PA0, 512)
                    for j in range(16):
                        if j == 0:
                            ts(a512, cbv.c(2, 514), cst(C_CW + i * 31), cst(C_CB + i), ALU.mult, ALU.add, extra_rd=[CST])
                        else:
                            stt(a512, cbv.c(2 + j, 514 + j), cst(C_CW + i * 31 + j), a512, ALU.mult, ALU.add, extra_rd=[CST])
                        jb = 16 + j
                        if jb == 16:
                            ts(accb, cbv.c(2 + jb, 514 + jb), cst(C_CW + i * 31 + jb), None, ALU.mult, extra_rd=[CST])
                        elif jb < 31:
                            stt(accb, cbv.c(2 + jb, 514 + jb), cst(C_CW + i * 31 + jb), accb, ALU.mult, ALU.add, extra_rd=[CST])
                    tt(a512, a512, accb, ALU.add)

        bs_ = newstat()
        bq_ = newstat()
        for i in range(6):
            a512 = ACCc(i).c(0, 512)
            s = SQf(cnt["sq"], 512)
            cnt["sq"] += 1
            act(s, a512, AF.Square)
            mm_group([PS(bq_, 0, 512)], [(0, ONESB.ap, s.ap)], rd=[ONESB, s], first=(i == 0), last=(i == 5))
            s2 = SQf(cnt["sq"], 512)
            cnt["sq"] += 1
            act(s2, a512, AF.Copy)
            mm_group([PS(bs_, 0, 512)], [(0, ONESB.ap, s2.ap)], rd=[ONESB, s2], first=(i == 0), last=(i == 5))
        mu = Af(MU0, 512)
        var = Af(MU0 + 512, 512)
        rr = RSf(cnt["rs"], 512)
        cnt["rs"] += 1
        ts(mu, PS(bs_, 0, 512), 1.0 / 768, None, ALU.mult)
        tt(var, mu, mu, ALU.mult)
        stt(var, PS(bq_, 0, 512), 1.0 / 768, var, ALU.mult, ALU.subtract)
        act(rr, var, AF.Sqrt, scale=1.0, bias=EPSC.ap[:, 0:1], extra_rd=[EPSC])
        recip(rr, rr)
        for i in range(6):
            tt(ACCc(i).c(0, 512), ACCc(i).c(0, 512), mu, ALU.subtract)
        for i in range(6):
            tt(ACCc(i).c(0, 512), ACCc(i).c(0, 512), rr, ALU.mult)
        for i in range(6):
            a512 = ACCc(i).c(0, 512)
            for sg in segs:
                a0 = sg["pd"] - 32
                t0, ln = sg["t0"], sg["len"]
                act(MIXf(10 + i, n).c(t0, t0 + ln), a512.c(a0, a0 + ln), AF.Silu,
                    scale=cst(C_LG + i), bias=cst(C_LB + i), extra_rd=[CST])

        st = Stat(n, D)
        woutn = wres("wout%d" % l, D, 256)
        mixall = [MIXc(k, n) for k in range(NKC)]
        for mg in range(NKC // 2):
            vw, aw = wload(woutS[l][:, mg * 256:(mg + 1) * 256], NKC, woutn)
            b = [newbank(), newbank()]
            outs = [PSn(b[0], n), PSn(b[1], n)]
            pairs = []
            for k in range(NKC):
                for mm in range(2):
                    pairs.append((mm, aw[:, k, mm * 128:(mm + 1) * 128], mixall[k].ap))
            mm_group(outs, pairs, rd=[vw] + mixall)
            for mm in range(2):
                m = 2 * mg + mm
                st.add(outs[mm], m == NKC - 1)
                cp(Dc(m, n), outs[mm])
        postnorm(st, 3, n, 1.0, pk)

    def load_consts(l, tile):
        dma("sp", Af(CST0, NCST), DR(cst_d[l], "cst_d"))
        if CASTALL:
            dma("pool", V(Ab(PWB0, 512).ap.rearrange("p (g d) -> p g d", d=128), Ab(PWB0, 512).res),
                DR(pw_d[l].rearrange("g c d -> c g d"), "pw_src"))
        else:
            dma("sp", V(Ab(PWB0, 512).ap.rearrange("p (g d) -> p g d", d=128), Ab(PWB0, 512).res),
                DR(pwb_s[l].rearrange("g c d -> c g d"), ("pw%d" % l, 0)))
        swt = Af(SWT0, 768)
        dma("sp", swt, DR(swT_d[l].rearrange("k h q -> k (h q)"), "swT_d"))
        dma("sp", Af(BR0, 768, 1), DR(sb_d[l], "sb_d"))
        wmt = Ab(WMT0, 768)
        for h in range(6):
            tt(wmt.c(h * 128, (h + 1) * 128), swt.c(h * 128, (h + 1) * 128), MASK, ALU.mult)
        if tile.get("samp"):
            sws = Af(SWS0, 384, 64)
            sws3 = sws.ap.rearrange("p (h q) -> p h q", q=64)
            dma("sp", V(sws3[0:32, :, 0:32], sws.res), DR(swT_d[l, 0:32, :, 0:32], "swT_d"))
            dma("sp", V(sws3[32:64, :, 32:64], sws.res), DR(swT_d[l, 0:32, :, 0:32], "swT_d"))
            wms = Ab(WMS0, 384, 64)
            for h in range(6):
                tt(wms.c(h * 64, (h + 1) * 64), sws.c(h * 64, (h + 1) * 64), V(MASK.ap[0:64, 0:64], MASK.res), ALU.mult)
            br2 = Af(BR20, 384, 1)
            for s_ in range(2):
                dma("sp", V(br2.ap.rearrange("p (h s t) -> p h s t", s=2, t=32)[:, :, s_, :], br2.res),
                    DR(sb_d[l].rearrange("o (h q) -> o h q", q=128)[:, :, 0:32], "sb_d"))
            for s_ in range(2):
                stc = Af(STC0, 384)
                d3 = stc.ap.rearrange("p (i s t) -> p i s t", s=2, t=32)
                dma("sp", V(d3[:, :, s_, 2:32], stc.res), DR(scT[l, s_].rearrange("(i p) t -> p i t", p=128), "scT"))
                stp = Af(STP0, 128)
                d3p = stp.ap.rearrange("p (g s t) -> p g s t", s=2, t=16)
                dma("sp", V(d3p[:, :, s_, 1:16], stp.res), DR(spT[l, s_].rearrange("(g p) t -> p g t", p=128), "spT"))

    tiles = []
    ch4 = [(0, 128), (128, 128), (256, 128), (384, 128)]
    if LAY2:
        tiles.append(dict(n=384, warm=True, first_pd=320,
                          segs=[dict(kind="wu", t0=0, len=256, pd=32), dict(kind="pr0", t0=256, len=128, pd=320)],
                          tcs=[(0, 128), (128, 128), (256, 128)],
                          xl=[(0, 384, "x", 0)], yst=[(256, 128, "y", 0)]))
        nmid = (TPC - 512) // 512
        for t in range(nmid):
            tiles.append(dict(n=512, segs=[dict(kind="pr", t0=0, len=512, pd=32)], tcs=ch4,
                              xl=[(0, 512, "x", 384 + 512 * t)], yst=[(0, 512, "y", 128 + 512 * t)]))
        tiles.append(dict(n=448, samp=True,
                          segs=[dict(kind="pr", t0=0, len=384, pd=32),
                                dict(kind="s", s=0, t0=WU, len=32, pd=32 + WU + 32),
                                dict(kind="s", s=1, t0=WU + 32, len=32, pd=32 + WU + 32 + 64)],
                          tcs=[(0, 128), (128, 128), (256, 128), (384, 64)],
                          xl=[(0, 384, "x", 384 + 512 * nmid), (384, 64, "xs", 0)],
                          yst=[(0, 384, "y", 128 + 512 * nmid), (384, 64, "ys", 0)]))
    else:
        tiles.append(dict(n=WU + 64, warm=True, samp=True,
                          segs=[dict(kind="wu", t0=0, len=WU, pd=32),
                                dict(kind="s", s=0, t0=WU, len=32, pd=32 + WU + 32),
                                dict(kind="s", s=1, t0=WU + 32, len=32, pd=32 + WU + 32 + 64)],
                          tcs=[(0, 128), (128, 128), (256, 128), (384, 64)],
                          xl=[(0, WU, "x", 0), (WU, 64, "xs", 0)], yst=[(WU, 64, "ys", 0)]))
        for t in range(NPT):
            tiles.append(dict(n=512, first_pd=(32 if t == 0 else None), segs=[dict(kind="pr", t0=0, len=512, pd=32)],
                              tcs=ch4, xl=[(0, 512, "x", WU + 512 * t)], yst=[(0, 512, "y", 512 * t)]))
    if cfg.get("tiles") is not None:
        tiles = [tiles[i] for i in cfg["tiles"]]

    for ti_, tile in enumerate(tiles):
        n = tile["n"]
        C0[0] = 0
        for k in range(NKC):
            for (xc, ncol, src, sc) in tile["xl"]:
                srcap = xT if src == "x" else xsT
                dma("sp", Af(X0 + k * 512 + xc, ncol), DR(srcap[k * 128:(k + 1) * 128, sc:sc + ncol], src + "T"))
        for l in range(NL):
            C0[0] = 0
            tile_l = tile
            if tile.get("warm") and cfg.get("shrink", True):
                dd = NL - 1 - l
                tl = dict(tile)
                if LAY2:
                    c0 = 224 if dd == 0 else (128 if dd == 1 else 0)
                    tl["segs"] = [dict(kind="wu", t0=c0, len=256 - c0, pd=32 + c0)] + tile["segs"][1:]
                    tl["tcs"] = [(t_, 128) for t_ in range((c0 + 127) // 128 * 128, 384, 128)]
                else:
                    c0 = 352 if dd == 0 else (256 if dd == 1 else (128 if dd <= 3 else 0))
                    tl["segs"] = [dict(kind="wu", t0=c0, len=WU - c0, pd=32 + c0)] + tile["segs"][1:]
                    tl["tcs"] = [(t_, 128) for t_ in range((c0 + 127) // 128 * 128, WU, 128)] + [(WU, 64)]
                C0[0] = c0
                tile_l = tl
            load_consts(l, tile)
            stg = cfg.get("stages", ("f0", "mx", "f1"))
            WQ["q"] = "pool" if (tile.get("warm") or CASTALL) else "sp"
            if "f0" in stg:
                stage_begin(l * 3, 68)
                ffn(l, 0, n, False)
                stage_end()
            if "mx" in stg:
                stage_begin(l * 3 + 1, 22)
                mixer(l, tile_l)
                stage_end()
            if "f1" in stg:
                stage_begin(l * 3 + 2, 68)
                ffn(l, 1, n, False)
                stage_end()
        C0[0] = 0
        for k in range(NKC):
            for (xc, ncol, dst, dc) in tile["yst"]:
                dstap = yT if dst == "y" else ysT
                dma("sp", DR(dstap[k * 128:(k + 1) * 128, dc:dc + ncol], (dst, k, ti_)), Af(X0 + k * 512 + xc, ncol))
    for l in range(NL):
        dma("sp", DR(nppT[l].rearrange("(g p) t -> p g t", p=128), ("npp", l)),
            V(Af(HP0 + l * 64, 64).ap.rearrange("p (g t) -> p g t", t=16), Af(HP0 + l * 64, 64).res))
        dma("sp", DR(ncpT[l].rearrange("(i p) t -> p i t", p=128), ("ncp", l)),
            V(Af(HC0 + l * 192, 192).ap.rearrange("p (i t) -> p i t", t=32), Af(HC0 + l * 192, 192).res))

    P.emit(nc, es)
    es.close()
    return nc


def _prep_common(inp, NL):
    f = np.float32
    ng = np.asarray(inp["norm_g"], f)[:NL]
    cst = np.zeros((NL, 128, NCST), f)
    cst[:, :, 0:96] = ng.reshape(NL, 6, 16, 128).transpose(0, 3, 1, 2).reshape(NL, 128, 96)
    cst[:, :, 96:100] = np.asarray(inp["pool_scale"], f)[:NL].reshape(NL, 4, 128).transpose(0, 2, 1)
    cst[:, :, 100:106] = np.asarray(inp["sgu_norm_g"], f)[:NL].reshape(NL, 6, 128).transpose(0, 2, 1)
    cw = np.asarray(inp["conv_w"], f)[:NL]
    cst[:, :, 106:292] = cw.reshape(NL, 31, 6, 128).transpose(0, 3, 2, 1).reshape(NL, 128, 186)
    cst[:, :, 292:298] = np.asarray(inp["conv_b"], f)[:NL].reshape(NL, 6, 128).transpose(0, 2, 1)
    cst[:, :, 298:304] = np.asarray(inp["conv_ln_g"], f)[:NL].reshape(NL, 6, 128).transpose(0, 2, 1)
    cst[:, :, 304:310] = np.asarray(inp["conv_ln_b"], f)[:NL].reshape(NL, 6, 128).transpose(0, 2, 1)
    sw = np.asarray(inp["sgu_w"], f)[:NL]
    swT = np.ascontiguousarray(sw.transpose(0, 3, 1, 2))
    sb = np.ascontiguousarray(np.asarray(inp["sgu_b"], f)[:NL].reshape(NL, 1, 768))
    k = np.arange(128)
    mask = (k[:, None] <= k[None, :]).astype(f)
    ident = np.eye(128, dtype=f)
    return dict(cst=cst, swT=swT, sb=sb, mask=mask, ident=ident,
                wg=np.ascontiguousarray(np.asarray(inp["ffn_w_gate"], f)[:NL]),
                wu=np.ascontiguousarray(np.asarray(inp["ffn_w_up"], f)[:NL]),
                wd=np.ascontiguousarray(np.asarray(inp["ffn_w_down"], f)[:NL]),
                win=np.ascontiguousarray(np.asarray(inp["w_in"], f)[:NL]),
                wout=np.ascontiguousarray(np.asarray(inp["w_out"], f)[:NL]),
                pw=np.ascontiguousarray(np.asarray(inp["pool_w"], f)[:NL]))


def run(inp, cfg):
    NL, TPC, NCO = cfg["NL"], cfg["TPC"], cfg["NCORES"]
    f = np.float32
    common = _prep_common(inp, NL)
    xp = np.asarray(inp["x_prompt"], f)[0]
    xs = np.asarray(inp["x_sample"], f)
    stp = np.asarray(inp["state_pool"], f)[:NL]
    stc = np.asarray(inp["state_conv"], f)[:NL]
    in_maps = []
    for c in range(NCO):
        m = dict(common)
        WUR = 256 if cfg.get("lay2", True) else WU
        lo = c * TPC - WUR
        xin = np.zeros((WUR + TPC, D), f)
        if c == 0:
            xin[WUR:] = xp[0:TPC]
        else:
            xin[:] = xp[lo:lo + WUR + TPC]
        m["xT"] = np.ascontiguousarray(xin.T)
        m["xsT"] = np.ascontiguousarray(xs[2 * c:2 * c + 2].reshape(64, D).T)
        m["spT"] = np.ascontiguousarray(stp[:, 2 * c:2 * c + 2].transpose(0, 1, 3, 2))
        m["scT"] = np.ascontiguousarray(stc[:, 2 * c:2 * c + 2].transpose(0, 1, 3, 2))
        m["flag"] = np.full((128, 1), 0.0 if c == 0 else 1.0, f)
        pos = np.arange(16) + c * TPC
        inv = np.zeros((128, 64), f)
        for g, w in enumerate(POOL_W):
            inv[:, g * 16:(g + 1) * 16] = (1.0 / np.minimum(w, pos + 1).astype(f))[None, :]
        m["invc"] = inv
        in_maps.append(m)
    nc = build(cfg)
    res = run_bass_kernel_spmd(nc, in_maps, core_ids=list(range(NCO)))
    R = res.results
    y_prompt = np.concatenate([r["yT"].T for r in R], axis=0)[None].astype(f)
    y_sample = np.concatenate([r["ysT"].T.reshape(2, 32, D) for r in R], axis=0).astype(f)
    last = R[NCO - 1]
    npp = np.ascontiguousarray(last["nppT"][:, :, 1:16].transpose(0, 2, 1))[:, None].astype(f)
    ncp = np.ascontiguousarray(last["ncpT"][:, :, 2:32].transpose(0, 2, 1))[:, None].astype(f)
    nps = np.concatenate([r["npsT"][:, :, :, 1:16].transpose(0, 1, 3, 2) for r in R], axis=1).astype(f)
    ncs = np.concatenate([r["ncsT"][:, :, :, 2:32].transpose(0, 1, 3, 2) for r in R], axis=1).astype(f)
    nv = np.concatenate([r["nvT"].transpose(0, 2, 1).reshape(NL, 2, 32, 768) for r in R], axis=1).astype(f)
    return (np.ascontiguousarray(y_prompt), np.ascontiguousarray(y_sample), npp, ncp,
            np.ascontiguousarray(nps), np.ascontiguousarray(ncs), np.ascontiguousarray(nv))


def kernel(**inputs):
    cfg = dict(NL=4, TPC=2048, NCORES=8)
    return run(inputs, cfg)
```
